# Optimizing a Trainium2 kernel written in Bass

```python
import math
import jax
import jax.numpy as jnp
from jax import lax
import numpy as np

D_MODEL = 1024
BATCH = 8
SEQ = 8192
DEPTH = 4

HEAD_DIM = 64
MIX_WIDTH = D_MODEL
A_HEADS = 4
DIFF_QK_DIM = HEAD_DIM // 2
DIFF_V_DIM = HEAD_DIM
B_HEADS = 6
C_HEADS = 6
DILATED_PATTERNS = ((128, 1), (512, 4), (2048, 16))
Q_BLOCK = 128
ROPE_THETA = 10000.0
D_FF = 2816
CONV_WIDTH = 3
NORM_EPS = 1e-6
SUBLN_EPS = 1e-5
FORGET_BIAS_INIT = 4.0

A_Q = A_HEADS * 2 * DIFF_QK_DIM
A_K = A_HEADS * 2 * DIFF_QK_DIM
A_V = A_HEADS * DIFF_V_DIM
B_W = B_HEADS * HEAD_DIM
C_W = C_HEADS * HEAD_DIM
IN_COLS = A_Q + A_K + A_V + 3 * B_W + 3 * C_W + C_HEADS

kernel_name = 'hybrid_diff_dilated_fox_block'


def rmsnorm(x, g, eps=NORM_EPS):
    xf = x.astype(jnp.float32)
    y = xf * lax.rsqrt(jnp.mean(xf * xf, axis=-1, keepdims=True) + eps)
    return (y * g.astype(jnp.float32)).astype(x.dtype)


def rope_tables(seq, dim):
    inv = 1.0 / (ROPE_THETA ** (jnp.arange(0, dim, 2, dtype=jnp.float32) / dim))
    ang = jnp.arange(seq, dtype=jnp.float32)[:, None] * inv[None, :]
    return jnp.cos(ang), jnp.sin(ang)


def apply_rope(x, cos, sin):
    half = x.shape[-1] // 2
    xf = x.astype(jnp.float32)
    x1, x2 = xf[..., :half], xf[..., half:]
    out = jnp.concatenate([x1 * cos - x2 * sin, x2 * cos + x1 * sin], axis=-1)
    return out.astype(x.dtype)


def diff_attention(q, k, v, lam):
    bn, h, _, t, dk = q.shape
    nb = t // Q_BLOCK
    qb = q.reshape(bn, h, 2, nb, Q_BLOCK, dk).transpose(3, 0, 1, 2, 4, 5)
    kpos = jnp.arange(t)
    scale = dk ** -0.5
    vf = v.astype(jnp.float32)

    def block(args):
        qblk, bi = args
        s = jnp.einsum('bhmqd,bhmkd->bhmqk', qblk, k).astype(jnp.float32) * scale
        qpos = bi * Q_BLOCK + jnp.arange(Q_BLOCK)
        s = jnp.where(kpos[None, :] <= qpos[:, None], s, -jnp.inf)
        p = jax.nn.softmax(s, axis=-1)
        a = p[:, :, 0] - lam * p[:, :, 1]
        return jnp.einsum('bhqk,bhkd->bhqd', a, vf)

    o = lax.map(block, (qb, jnp.arange(nb)))
    return o.transpose(1, 2, 0, 3, 4).reshape(bn, h, t, -1)


def forgetting_attention(q, k, v, cum_logf):
    bn, h, t, hd = q.shape
    nb = t // Q_BLOCK
    qb = q.reshape(bn, h, nb, Q_BLOCK, hd).transpose(2, 0, 1, 3, 4)
    fb = cum_logf.reshape(bn, h, nb, Q_BLOCK).transpose(2, 0, 1, 3)
    kpos = jnp.arange(t)
    scale = hd ** -0.5
    vf = v.astype(jnp.float32)

    def block(args):
        qblk, fq, bi = args
        s = jnp.einsum('bhqd,bhkd->bhqk', qblk, k).astype(jnp.float32) * scale
        s = s + fq[..., :, None] - cum_logf[..., None, :]
        qpos = bi * Q_BLOCK + jnp.arange(Q_BLOCK)
        s = jnp.where(kpos[None, :] <= qpos[:, None], s, -jnp.inf)
        p = jax.nn.softmax(s, axis=-1)
        return jnp.einsum('bhqk,bhkd->bhqd', p, vf)

    o = lax.map(block, (qb, fb, jnp.arange(nb)))
    return o.transpose(1, 2, 0, 3, 4).reshape(bn, h, t, hd)


def banded_window_attention(q, k, v, w):
    bn, g, l, hd = q.shape
    nb = l // w
    qb = q.reshape(bn, g, nb, w, hd)

    def with_prev(a):
        a = a.reshape(bn, g, nb, w, hd)
        prev = jnp.pad(a, ((0, 0), (0, 0), (1, 0), (0, 0), (0, 0)))[:, :, :-1]
        return jnp.concatenate([prev, a], axis=3)

    kk, vv = with_prev(k), with_prev(v)
    s = jnp.einsum('bgnqd,bgnkd->bgnqk', qb, kk).astype(jnp.float32) * (hd ** -0.5)
    dist = jnp.arange(w)[:, None] + w - jnp.arange(2 * w)[None, :]
    kidx = jnp.arange(nb)[:, None] * w + jnp.arange(2 * w)[None, :] - w
    mask = ((dist >= 0) & (dist <= w))[None, :, :] & (kidx >= 0)[:, None, :]
    s = jnp.where(mask, s, -jnp.inf)
    m = jnp.max(s, axis=-1, keepdims=True)
    p = jnp.exp(s - m)
    den = jnp.sum(p, axis=-1, keepdims=True)
    o = jnp.einsum('bgnqk,bgnkd->bgnqd', p, vv.astype(jnp.float32)) / den
    lse = (m + jnp.log(den))[..., 0]
    return o.reshape(bn, g, l, hd), lse.reshape(bn, g, l)


def dilated_mixture_attention(q, k, v):
    bn, h, t, hd = q.shape
    outs, lses = [], []
    for window, d in DILATED_PATTERNS:
        l = t // d
        w = window // d
        lp = -(-l // w) * w

        def fold(a):
            a = a.reshape(bn, h, l, d, hd).transpose(0, 1, 3, 2, 4).reshape(bn, h * d, l, hd)
            return jnp.pad(a, ((0, 0), (0, 0), (0, lp - l), (0, 0)))

        o, lse = banded_window_attention(fold(q), fold(k), fold(v), w)
        o = o[:, :, :l].reshape(bn, h, d, l, hd).transpose(0, 1, 3, 2, 4).reshape(bn, h, t, hd)
        lse = lse[:, :, :l].reshape(bn, h, d, l).transpose(0, 1, 3, 2).reshape(bn, h, t)
        outs.append(o)
        lses.append(lse)
    wts = jax.nn.softmax(jnp.stack(lses, axis=0), axis=0)
    return jnp.sum(wts[..., None] * jnp.stack(outs, axis=0), axis=0)


def causal_depthwise_conv(g, w, b):
    t = g.shape[1]
    gp = jnp.pad(g, ((0, 0), (CONV_WIDTH - 1, 0), (0, 0)))
    y = b
    for i in range(CONV_WIDTH):
        y = y + gp[:, i:i + t] * w[i]
    return y


def mixer_layer(h, w_in, w_out, lam_params, subln_g, forget_bias, lam_init, cos_a, sin_a, cos_b, sin_b):
    bn, t, _ = h.shape
    proj = h @ w_in
    cuts = np.cumsum([A_Q, A_K, A_V, B_W, B_W, B_W, C_W, C_W, C_W])
    qa, ka, va, qb, kb, vb, qc, kc, vc, fz = jnp.split(proj, cuts, axis=-1)

    qa = apply_rope(qa.reshape(bn, t, A_HEADS, 2, DIFF_QK_DIM).transpose(0, 2, 3, 1, 4), cos_a, sin_a)
    ka = apply_rope(ka.reshape(bn, t, A_HEADS, 2, DIFF_QK_DIM).transpose(0, 2, 3, 1, 4), cos_a, sin_a)
    va = va.reshape(bn, t, A_HEADS, DIFF_V_DIM).transpose(0, 2, 1, 3)
    lp = lam_params.astype(jnp.float32)
    lam = jnp.exp(jnp.sum(lp[0] * lp[1])) - jnp.exp(jnp.sum(lp[2] * lp[3])) + lam_init
    oa = diff_attention(qa, ka, va, lam)
    oa = oa * lax.rsqrt(jnp.mean(oa * oa, axis=-1, keepdims=True) + SUBLN_EPS)
    oa = oa * subln_g.astype(jnp.float32) * (1.0 - lam_init)

    def heads(a, n):
        return a.reshape(bn, t, n, HEAD_DIM).transpose(0, 2, 1, 3)
    qb = apply_rope(heads(qb, B_HEADS), cos_b, sin_b)
    kb = apply_rope(heads(kb, B_HEADS), cos_b, sin_b)
    ob = dilated_mixture_attention(qb, kb, heads(vb, B_HEADS))

    logf = jax.nn.log_sigmoid((fz + forget_bias).astype(jnp.float32))
    cum_logf = lax.cumsum(logf, axis=1).transpose(0, 2, 1)
    oc = forgetting_attention(heads(qc, C_HEADS), heads(kc, C_HEADS), heads(vc, C_HEADS), cum_logf)

    def merge(o):
        return o.transpose(0, 2, 1, 3).reshape(bn, t, -1)
    o = jnp.concatenate([merge(oa), merge(ob), merge(oc)], axis=-1).astype(h.dtype)
    return o @ w_out


def setup_inputs(seed: int = 0) -> dict:
    key = jax.random.key(seed)
    ks = jax.random.split(key, 18)
    f32 = jnp.float32
    n = jax.random.normal
    return {
        'x': n(ks[0], (BATCH, SEQ, D_MODEL), f32),
        'c': n(ks[1], (BATCH, D_MODEL), f32),
        'w_mod': n(ks[2], (DEPTH, D_MODEL, 6 * D_MODEL), f32) * (0.5 * D_MODEL ** -0.5),
        'b_mod': n(ks[3], (DEPTH, 6 * D_MODEL), f32) * 0.02,
        'g_attn': 1.0 + 0.02 * n(ks[4], (DEPTH, D_MODEL), f32),
        'w_in': n(ks[5], (DEPTH, D_MODEL, IN_COLS), f32) * D_MODEL ** -0.5,
        'diff_lambda': n(ks[6], (DEPTH, 4, DIFF_QK_DIM), f32) * 0.1,
        'subln_g': 1.0 + 0.02 * n(ks[7], (DEPTH, DIFF_V_DIM), f32),
        'forget_bias': FORGET_BIAS_INIT + 0.5 * n(ks[8], (DEPTH, C_HEADS), f32),
        'w_out': n(ks[9], (DEPTH, MIX_WIDTH, D_MODEL), f32) * MIX_WIDTH ** -0.5,
        'g_mlp': 1.0 + 0.02 * n(ks[10], (DEPTH, D_MODEL), f32),
        'w_up': n(ks[11], (DEPTH, D_MODEL, 2 * D_FF), f32) * D_MODEL ** -0.5,
        'conv_w': n(ks[12], (DEPTH, CONV_WIDTH, D_FF), f32) * CONV_WIDTH ** -0.5,
        'conv_b': n(ks[13], (DEPTH, D_FF), f32) * 0.02,
        'w_down': n(ks[14], (DEPTH, D_FF, D_MODEL), f32) * D_FF ** -0.5,
        'g_final': 1.0 + 0.02 * n(ks[15], (D_MODEL,), f32),
    }


def reference(x, c, w_mod, b_mod, g_attn, w_in, diff_lambda, subln_g, forget_bias, w_out,
              g_mlp, w_up, conv_w, conv_b, w_down, g_final):
    t = x.shape[1]
    cos_a, sin_a = rope_tables(t, DIFF_QK_DIM)
    cos_b, sin_b = rope_tables(t, HEAD_DIM)
    sc = jax.nn.silu(c)
    for layer in range(DEPTH):
        lam_init = 0.8 - 0.6 * math.exp(-0.3 * layer)
        mod = sc @ w_mod[layer] + b_mod[layer]
        shift1, scale1, gate1, shift2, scale2, gate2 = [m[:, None, :] for m in jnp.split(mod, 6, axis=-1)]

        h = rmsnorm(x, g_attn[layer]) * (1.0 + scale1) + shift1
        o = mixer_layer(h, w_in[layer], w_out[layer], diff_lambda[layer], subln_g[layer],
                        forget_bias[layer], lam_init, cos_a, sin_a, cos_b, sin_b)
        x = x + gate1 * o

        h = rmsnorm(x, g_mlp[layer]) * (1.0 + scale2) + shift2
        up = h @ w_up[layer]
        u, g = up[..., :D_FF], up[..., D_FF:]
        g = causal_depthwise_conv(g, conv_w[layer], conv_b[layer])
        x = x + gate2 * ((jax.nn.silu(g) * u) @ w_down[layer])
    return rmsnorm(x, g_final)
```

```python
import contextlib
import math

import ml_dtypes
import numpy as np

import concourse.bass as bass
import concourse.mybir as mybir
from concourse.bass_utils import run_bass_kernel_spmd

F32 = mybir.dt.float32
BF16 = mybir.dt.bfloat16
AF = mybir.ActivationFunctionType
ALU = mybir.AluOpType
AX = mybir.AxisListType

D = 1024
DFF = 2816
NFF = 22
INC = 3078
ROTC = 1280
NORM_EPS = 1e-6
SUBLN_EPS = 1e-5
NCORES = 8
SAME_ENGINE_SYNC = True

QA0, KA0, VA0, QB0, KB0, VB0, QC0, KC0, VC0, FZ0 = 0, 256, 512, 768, 1152, 1536, 1920, 2304, 2688, 3072


class Tracker:
    def __init__(self, nc, es):
        self.nc = nc
        self.E = dict(pe=nc.tensor, act=nc.scalar, dve=nc.vector, pool=nc.gpsimd, sp=nc.sync)
        self.sem = {}
        self.cnt = {}
        self.es = es
        for e in ("pe", "act", "dve", "pool"):
            self.sem[e] = es.enter_context(nc.semaphore("prog_" + e))
            self.cnt[e] = 0
        self.waited = {e: {} for e in self.E}
        self.bw = {}
        self.br = {}
        self.ninst = 0

    def dsem(self, name):
        if name not in self.sem:
            self.sem[name] = self.es.enter_context(self.nc.semaphore("d_" + name))
            self.cnt[name] = 0
        return name

    def _deps(self, r, w):
        d = {}

        def add(k, v):
            if k not in self.E:
                v = self.cnt[k]
            if d.get(k, 0) < v:
                d[k] = v

        for b in r:
            t = self.bw.get(b)
            if t:
                add(*t)
        for b in w:
            t = self.bw.get(b)
            if t:
                add(*t)
            for k, v in self.br.get(b, {}).items():
                add(k, v)
        return d

    def _waits(self, e, d):
        for k, v in d.items():
            if self.waited[e].get(k, 0) >= v:
                continue
            if k == e and (e == "pe" or not SAME_ENGINE_SYNC):
                continue
            self.E[e].wait_ge(self.sem[k], v)
            self.waited[e][k] = v
            self.ninst += 1

    def _record(self, tok, r, w):
        k, v = tok
        for b in r:
            self.br.setdefault(b, {})[k] = v
        for b in w:
            self.bw[b] = tok
            self.br[b] = {}

    def op(self, e, fn, r=(), w=()):
        self._waits(e, self._deps(r, w))
        inst = fn(self.E[e])
        self.cnt[e] += 1
        inst.then_inc(self.sem[e], 1)
        self.ninst += 1
        self._record((e, self.cnt[e]), r, w)

    def dma(self, q, out, in_, sem, r=(), w=(), **kw):
        self.dsem(sem)
        self._waits(q, self._deps(r, w))
        inst = self.E[q].dma_start(out=out, in_=in_, **kw)
        self.cnt[sem] += 16
        inst.then_inc(self.sem[sem], 16)
        self.ninst += 1
        self._record((sem, self.cnt[sem]), r, w)

    def barrier(self, engines=("pe", "act", "dve", "pool", "sp")):
        for e in engines:
            for k, v in self.cnt.items():
                if v == 0 or k == e:
                    continue
                if self.waited[e].get(k, 0) >= v:
                    continue
                self.E[e].wait_ge(self.sem[k], v)
                self.waited[e][k] = v
        self.bw = {}
        self.br = {}


def build(T, L, final=True, dbg=False):
    NQ = T // 512
    NT = T // 128
    nc = bass.Bass("TRN2", target_bir_lowering=False)

    def din(name, shape, dt=F32):
        return nc.dram_tensor(name, list(shape), dt, kind="ExternalInput").ap()

    def dscr(name, shape, dt):
        kind = "ExternalOutput" if dbg else "Internal"
        return nc.dram_tensor(name, list(shape), dt, kind=kind).ap()

    x_d = din("x", [T, D])
    c_d = din("c", [D])
    w_mod = din("w_mod", [L, D, 6 * D])
    b_mod = din("b_mod", [L, 6 * D])
    g_attn = din("g_attn", [L, D])
    w_in = din("w_in", [L, D, INC])
    w_inr = din("w_inr", [L, D, ROTC])
    dlam = din("diff_lambda", [L, 128])
    subg = din("subln_g", [L, 64])
    fbias = din("forget_bias", [L, 6])
    w_out = din("w_out", [L, D, D])
    g_mlp = din("g_mlp", [L, D])
    w_up = din("w_up", [L, D, 2 * DFF])
    conv_w = din("conv_w", [L, 3, DFF])
    conv_b = din("conv_b", [L, DFF])
    w_down = din("w_down", [L, DFF, D])
    g_fin = din("g_final", [1, D])
    lamc = din("lamc", [L, 2])
    ropeA = din("ropeA", [2, 32, T])
    ropeB = din("ropeB", [2, 64, T])
    maskB_d = din("maskB", [20, 128, 512], BF16)
    tri_d = din("tri", [128, 128], BF16)
    ident_d = din("identb", [128, 128], BF16)

    y_d = nc.dram_tensor("y", [T, D], F32, kind="ExternalOutput").ap()

    xres = dscr("xres", [T, D], F32)
    qaT = dscr("qaT", [256, T], BF16)
    kaT = dscr("kaT", [256, T], BF16)
    qbT = dscr("qbT", [384, T], BF16)
    kbT = dscr("kbT", [384, T], BF16)
    qcT = dscr("qcT", [6, 68, T], BF16)
    kcT = dscr("kcT", [6, 68, T], BF16)
    va_d = dscr("va", [T, 256], BF16)
    vb_d = dscr("vb", [T, 384], BF16)
    vc_d = dscr("vc", [T, 384], BF16)
    oT_d = dscr("oT", [D, T], BF16)
    aT_d = dscr("aT", [DFF, T], BF16)

    top = contextlib.ExitStack()
    with top:
        tr = Tracker(nc, top)

        uniq = [0]

        def sbuf(es, name, shape, dt):
            uniq[0] += 1
            return es.enter_context(nc.sbuf_tensor(f"{name}_u{uniq[0]}", list(shape), dt))

        ps = top.enter_context(nc.psum_tensor("ps", [128, 8, 512], F32))

        def PS(b):
            return ps[:, b, :]

        def PSB(b):
            return ps[:, b, :].bitcast(BF16)

        mod = sbuf(top, "mod", [128, 6 * D], F32)
        SC = sbuf(top, "SC", [128, 8, 128], F32)
        identb = sbuf(top, "identb_s", [128, 128], BF16)
        tri = sbuf(top, "tri_s", [128, 128], BF16)
        ones32 = sbuf(top, "ones32", [128, 128], F32)
        onesb = sbuf(top, "onesb", [128, 512], BF16)
        sm = sbuf(top, "sm", [128, 16], F32)
        lamt = sbuf(top, "lamt", [128, 2], F32)

        tr.dma("sp", identb[:], ident_d[:, :], "c0", w=["identb"])
        tr.dma("sp", tri[:], tri_d[:, :], "c0", w=["tri"])
        tr.op("dve", lambda e: e.memset(ones32[:], 1.0), w=["ones32"])
        tr.op("dve", lambda e: e.memset(onesb[:], 1.0), w=["onesb"])

        with contextlib.ExitStack() as es:
            ct = sbuf(es, "ct", [128, 8], F32)
            sct = sbuf(es, "sct", [128, 8], F32)
            tr.dma("sp", ct[:], c_d.rearrange("(kc p) -> p kc", p=128), "c0", w=["ct"],
                   allow_slow_non_contiguous=True)
            tr.op("act", lambda e: e.activation(out=sct[:], in_=ct[:], func=AF.Silu), r=["ct"], w=["sct"])
            for kc in range(8):
                tr.op("dve", lambda e, kc=kc: e.tensor_scalar(
                    out=SC[:, kc, :], in0=ones32[:, :], scalar1=sct[:, kc:kc + 1], scalar2=None, op0=ALU.mult),
                    r=["sct", "ones32"], w=["SC"])
            tr.barrier()

        def phase0(l):
            with contextlib.ExitStack() as es:
                wm = [sbuf(es, f"wm{i}", [128, 8, 512], F32) for i in range(2)]
                bm = [sbuf(es, f"bm{i}", [128, 512], F32) for i in range(2)]
                gb = sbuf(es, "gb", [128, D], F32)
                dl = sbuf(es, "dl", [64, 128], F32)
                tmp = sbuf(es, "tmp0", [64, 64], F32)
                wv = w_mod[l].rearrange("(kc p) n -> p kc n", p=128)
                for j in range(12):
                    s = j % 2
                    tr.dma("sp", wm[s][:], wv[:, :, j * 512:(j + 1) * 512], f"wm{s}", w=[f"wm{s}"])
                    tr.dma("pool", bm[s][:], b_mod[l:l + 1, j * 512:(j + 1) * 512].partition_broadcast(128),
                           f"bm{s}", w=[f"bm{s}"])
                    b = j % 2
                    for kc in range(8):
                        tr.op("pe", lambda e, kc=kc, s=s, b=b: e.matmul(
                            PS(b), lhsT=SC[:, kc, :], rhs=wm[s][:, kc, :], start=(kc == 0), stop=(kc == 7)),
                            r=[f"wm{s}", "SC"], w=[f"ps{b}"])
                    tr.op("dve", lambda e, s=s, b=b, j=j: e.tensor_tensor(
                        out=mod[:, j * 512:(j + 1) * 512], in0=PS(b), in1=bm[s][:], op=ALU.add),
                        r=[f"ps{b}", f"bm{s}"], w=["mod"])
                for (gsrc, off) in ((g_attn, D), (g_mlp, 4 * D)):
                    tr.dma("sp", gb[:], gsrc[l:l + 1, :].partition_broadcast(128), "c0", w=["gb"])
                    tr.op("dve", lambda e, off=off: e.scalar_tensor_tensor(
                        out=mod[:, off:off + D], in0=mod[:, off:off + D], scalar=1.0, in1=gb[:],
                        op0=ALU.add, op1=ALU.mult), r=["gb", "mod"], w=["mod"])
                tr.dma("sp", dl[:], dlam[l:l + 1, :].partition_broadcast(64), "c0", w=["dl"])
                tr.dma("sp", lamt[0:64, :], lamc[l:l + 1, :].partition_broadcast(64), "c0", w=["lamt"])
                tr.dma("sp", sm[0:64, 1:2], subg[l:l + 1, :].rearrange("o d -> d o"), "c0", w=["sm"])
                tr.dma("sp", sm[0:6, 2:3], fbias[l:l + 1, :].rearrange("o d -> d o"), "c0", w=["sm"])
                tr.op("dve", lambda e: e.tensor_tensor(out=tmp[:, 0:32], in0=dl[:, 0:32], in1=dl[:, 32:64], op=ALU.mult),
                      r=["dl"], w=["tmp0"])
                tr.op("dve", lambda e: e.tensor_tensor(out=tmp[:, 32:64], in0=dl[:, 64:96], in1=dl[:, 96:128], op=ALU.mult),
                      r=["dl", "tmp0"], w=["tmp0"])
                tr.op("dve", lambda e: e.tensor_reduce(out=sm[0:64, 3:4], in_=tmp[:, 0:32], axis=AX.X, op=ALU.add),
                      r=["tmp0", "sm"], w=["sm"])
                tr.op("dve", lambda e: e.tensor_reduce(out=sm[0:64, 4:5], in_=tmp[:, 32:64], axis=AX.X, op=ALU.add),
                      r=["tmp0", "sm"], w=["sm"])
                tr.op("act", lambda e: e.activation(out=sm[0:64, 5:7], in_=sm[0:64, 3:5], func=AF.Exp), r=["sm"], w=["sm"])
                tr.op("dve", lambda e: e.tensor_tensor(out=sm[0:64, 7:8], in0=sm[0:64, 6:7], in1=sm[0:64, 5:6], op=ALU.subtract),
                      r=["sm"], w=["sm"])
                tr.op("dve", lambda e: e.tensor_tensor(out=sm[0:64, 0:1], in0=sm[0:64, 7:8], in1=lamt[0:64, 0:1], op=ALU.subtract),
                      r=["sm", "lamt"], w=["sm"])
                tr.op("dve", lambda e: e.tensor_tensor(out=sm[0:64, 1:2], in0=sm[0:64, 1:2], in1=lamt[0:64, 1:2], op=ALU.mult),
                      r=["sm", "lamt"], w=["sm"])
                tr.op("dve", lambda e: e.tensor_scalar(out=sm[0:6, 2:3], in0=sm[0:6, 2:3], scalar1=-1.0, scalar2=None, op0=ALU.mult),
                      r=["sm"], w=["sm"])
                tr.barrier()

        def rms_to_hT(es_bufs, xin_t, xkey, s, gmod_off, shift_off, hT, hTkey, psb, tag):
            junk, ss, t32, hb = es_bufs
            tr.op("act", lambda e: e.activation(out=junk[:], in_=xin_t, func=AF.Square, accum_out=ss[:, 0:1]),
                  r=[xkey], w=["junk", "ss"])
            tr.op("dve", lambda e: e.tensor_scalar(out=ss[:, 1:2], in0=ss[:, 0:1], scalar1=1.0 / D, scalar2=NORM_EPS,
                                                   op0=ALU.mult, op1=ALU.add), r=["ss"], w=["ss"])
            tr.op("act", lambda e: e.activation(out=ss[:, 2:3], in_=ss[:, 1:2], func=AF.Sqrt), r=["ss"], w=["ss"])
            tr.op("dve", lambda e: e.reciprocal(out=ss[:, 3:4], in_=ss[:, 2:3]), r=["ss"], w=["ss"])
            tr.op("dve", lambda e: e.scalar_tensor_tensor(out=t32[:], in0=xin_t, scalar=ss[:, 3:4],
                                                          in1=mod[:, gmod_off:gmod_off + D], op0=ALU.mult, op1=ALU.mult),
                  r=[xkey, "ss", "mod"], w=["t32"])
            tr.op("pool", lambda e: e.tensor_tensor(out=hb[:], in0=t32[:], in1=mod[:, shift_off:shift_off + D], op=ALU.add),
                  r=["t32", "mod"], w=["hb"])
            for kc in range(8):
                tr.op("pe", lambda e, kc=kc: e.transpose(out=PSB(psb)[:, kc * 128:(kc + 1) * 128],
                                                         in_=hb[:, kc * 128:(kc + 1) * 128], identity=identb[:]),
                      r=["hb", "identb"], w=[f"ps{psb}"])
            tr.op("act", lambda e: e.activation(out=hT[:, :, s * 128:(s + 1) * 128],
                                                in_=PSB(psb).rearrange("p (kc t) -> p kc t", kc=8), func=AF.Copy),
                  r=[f"ps{psb}"], w=[hTkey])

        def load_w_bf16(dst, src_view, nk, key, sem):
            for kc in range(nk):
                tr.dma("pool", dst[:, kc, :], src_view[:, kc, :], sem, w=[key])

        def phase1(l, x_src):
            with contextlib.ExitStack() as es:
                win = sbuf(es, "win", [128, 8, INC], BF16)
                winr = sbuf(es, "winr", [128, 8, ROTC], BF16)
                xin = [sbuf(es, f"xin{i}", [128, D], F32) for i in range(3)]
                junk = sbuf(es, "junk", [128, D], BF16)
                ss = sbuf(es, "ss", [128, 4], F32)
                t32 = sbuf(es, "t32", [128, D], F32)
                hb = sbuf(es, "hb", [128, D], BF16)
                hT = [sbuf(es, f"hT{i}", [128, 8, 512], BF16) for i in range(2)]
                tabA = sbuf(es, "tabA", [128, 2, 512], F32)
                tabB = sbuf(es, "tabB", [128, 2, 512], F32)
                r1 = [sbuf(es, f"r1_{i}", [128, 512], F32) for i in range(2)]
                r2 = [sbuf(es, f"r2_{i}", [128, 512], F32) for i in range(2)]
                qst = [sbuf(es, f"qst{i}", [128, 512], BF16) for i in range(4)]
                vst = sbuf(es, "vst", [128, 4, 1024], BF16)
                fzt = sbuf(es, "fzt", [6, 8, 512], F32)
                fzb = sbuf(es, "fzb", [6, 4, 512], BF16)
                carry = sbuf(es, "carry", [6, 1], F32)

                load_w_bf16(win, w_in[l].rearrange("(kc p) n -> p kc n", p=128), 8, "win", "wl")
                load_w_bf16(winr, w_inr[l].rearrange("(kc p) n -> p kc n", p=128), 8, "winr", "wl")
                tr.op("dve", lambda e: e.memset(carry[:], 0.0), w=["carry"])
                tr.op("dve", lambda e: e.memset(fzt[:, 3, :], 1.0), w=["fzt3"])

                nx = 0
                nq = 0
                chunks = []
                for j in range(2):
                    chunks.append((QA0 + j * 128, j * 128, "A", qaT, j))
                for j in range(2):
                    chunks.append((KA0 + j * 128, 256 + j * 128, "A", kaT, j))
                for j in range(3):
                    chunks.append((QB0 + j * 128, 512 + j * 128, "B", qbT, j))
                for j in range(3):
                    chunks.append((KB0 + j * 128, 896 + j * 128, "B", kbT, j))
                for j in range(3):
                    chunks.append((QC0 + j * 128, None, "Cq", qcT, j))
                for j in range(3):
                    chunks.append((KC0 + j * 128, None, "Ck", kcT, j))

                for tt in range(NQ):
                    t0 = tt * 512
                    hs = tt % 2
                    hTt = hT[hs]
                    hkey = f"hT{hs}"
                    for s in range(4):
                        xs = nx % 3
                        nx += 1
                        tr.dma("sp", xin[xs][:], x_src[t0 + s * 128:t0 + (s + 1) * 128, :], f"xin{xs}", w=[f"xin{xs}"])
                        rms_to_hT((junk, ss, t32, hb), xin[xs][:], f"xin{xs}", s, D, 0, hTt, hkey, s % 2, "p1")
                    for u in range(4):
                        tr.dma("sp", tabA[u * 32:(u + 1) * 32, :, :],
                               ropeA[:, :, t0:t0 + 512].rearrange("c d t -> d c t"), "tab", w=["tabA"])
                    for u in range(2):
                        tr.dma("sp", tabB[u * 64:(u + 1) * 64, :, :],
                               ropeB[:, :, t0:t0 + 512].rearrange("c d t -> d c t"), "tab", w=["tabB"])
                    for ci, (c0, rc0, kind, dst, j) in enumerate(chunks):
                        b = 2 + (ci % 2)
                        for kc in range(8):
                            tr.op("pe", lambda e, kc=kc, c0=c0, b=b: e.matmul(
                                PS(b), lhsT=win[:, kc, c0:c0 + 128], rhs=hTt[:, kc, :], start=(kc == 0), stop=(kc == 7)),
                                r=["win", hkey], w=[f"ps{b}"])
                        qs = nq % 4
                        nq += 1
                        if rc0 is not None:
                            br_ = 4 + (ci % 2)
                            for kc in range(8):
                                tr.op("pe", lambda e, kc=kc, rc0=rc0, br_=br_: e.matmul(
                                    PS(br_), lhsT=winr[:, kc, rc0:rc0 + 128], rhs=hTt[:, kc, :], start=(kc == 0), stop=(kc == 7)),
                                    r=["winr", hkey], w=[f"ps{br_}"])
                            tab = tabA if kind == "A" else tabB
                            tkey = "tabA" if kind == "A" else "tabB"
                            rs = ci % 2
                            tr.op("dve", lambda e, b=b, rs=rs, tab=tab: e.tensor_tensor(
                                out=r1[rs][:], in0=PS(b), in1=tab[:, 0, :], op=ALU.mult),
                                r=[f"ps{b}", tkey], w=[f"r1_{rs}"])
                            tr.op("dve", lambda e, br_=br_, rs=rs, tab=tab: e.tensor_tensor(
                                out=r2[rs][:], in0=PS(br_), in1=tab[:, 1, :], op=ALU.mult),
                                r=[f"ps{br_}", tkey], w=[f"r2_{rs}"])
                            tr.op("pool", lambda e, rs=rs, qs=qs: e.tensor_tensor(
                                out=qst[qs][:], in0=r1[rs][:], in1=r2[rs][:], op=ALU.add),
                                r=[f"r1_{rs}", f"r2_{rs}"], w=[f"qst{qs}"])
                            tr.dma("sp", dst[j * 128:(j + 1) * 128, t0:t0 + 512], qst[qs][:], f"qst{qs}", r=[f"qst{qs}"])
                        else:
                            sc = 0.125 if kind == "Cq" else 1.0
                            tr.op("act", lambda e, b=b, qs=qs, sc=sc: e.activation(
                                out=qst[qs][:], in_=PS(b), func=AF.Copy, scale=sc),
                                r=[f"ps{b}"], w=[f"qst{qs}"])
                            for hh in range(2):
                                tr.dma("sp", dst[2 * j + hh, 0:64, t0:t0 + 512], qst[qs][hh * 64:(hh + 1) * 64, :],
                                       f"qst{qs}", r=[f"qst{qs}"])
                    for s in range(4):
                        for gi, (c0, n, o0) in enumerate(((VA0, 256, 0), (VB0, 384, 256), (VC0, 384, 640))):
                            b = 6 + ((s * 3 + gi) % 2)
                            for kc in range(8):
                                tr.op("pe", lambda e, kc=kc, c0=c0, n=n, b=b, s=s: e.matmul(
                                    PS(b)[:, 0:n], lhsT=hTt[:, kc, s * 128:(s + 1) * 128], rhs=win[:, kc, c0:c0 + n],
                                    start=(kc == 0), stop=(kc == 7)), r=["win", hkey], w=[f"ps{b}"])
                            tr.op("act", lambda e, b=b, n=n, o0=o0, s=s: e.activation(
                                out=vst[:, s, o0:o0 + n], in_=PS(b)[:, 0:n], func=AF.Copy), r=[f"ps{b}"], w=["vst"])
                    for (dst, n, o0) in ((va_d, 256, 0), (vb_d, 384, 256), (vc_d, 384, 640)):
                        tr.dma("sp", dst[t0:t0 + 512, :].rearrange("(s p) c -> p s c", p=128), vst[:, :, o0:o0 + n],
                               "vst", r=["vst"])
                    b = 6
                    for kc in range(8):
                        tr.op("pe", lambda e, kc=kc: e.matmul(
                            PS(6)[0:6, :], lhsT=win[:, kc, FZ0:FZ0 + 6], rhs=hTt[:, kc, :], start=(kc == 0), stop=(kc == 7)),
                            r=["win", hkey], w=["ps6"])
                    tr.op("act", lambda e: e.activation(out=fzt[:, 0, :], in_=PS(6)[0:6, :], func=AF.Exp,
                                                        scale=-1.0, bias=sm[0:6, 2:3]), r=["ps6", "sm"], w=["fzt0"])
                    tr.op("act", lambda e: e.activation(out=fzt[:, 1, :], in_=fzt[:, 0, :], func=AF.Ln, bias=1.0),
                          r=["fzt0"], w=["fzt1"])
                    tr.op("dve", lambda e: e.tensor_tensor_scan(out=fzt[:, 2, :], data0=fzt[:, 3, :], data1=fzt[:, 1, :],
                                                                initial=carry[:, 0:1], op0=ALU.mult, op1=ALU.subtract),
                          r=["fzt1", "fzt3", "carry", "fzb"], w=["fzt2"])
                    tr.op("dve", lambda e: e.tensor_copy(out=carry[:], in_=fzt[:, 2, 511:512]), r=["fzt2"], w=["carry"])
                    tr.op("dve", lambda e: e.tensor_copy(out=fzb[:, 0, :], in_=fzt[:, 2, :]), r=["fzt2"], w=["fzb"])
                    tr.op("dve", lambda e: e.tensor_tensor(out=fzb[:, 1, :], in0=fzt[:, 2, :], in1=fzb[:, 0, :], op=ALU.subtract),
                          r=["fzt2", "fzb"], w=["fzb"])
                    tr.op("dve", lambda e: e.tensor_scalar(out=fzb[:, 2:4, :], in0=fzb[:, 0:2, :], scalar1=-1.0, scalar2=None,
                                                           op0=ALU.mult), r=["fzb"], w=["fzb"])
                    for (dst, row, src) in ((qcT, 64, fzb[:, 0, :]), (qcT, 65, fzb[:, 1, :]), (qcT, 66, onesb[0:6, :]),
                                            (qcT, 67, onesb[0:6, :]), (kcT, 64, onesb[0:6, :]), (kcT, 65, onesb[0:6, :]),
                                            (kcT, 66, fzb[:, 2, :]), (kcT, 67, fzb[:, 3, :])):
                        tr.dma("sp", dst[:, row, t0:t0 + 512], src, "fz", r=["fzb", "onesb"])
                tr.barrier()

        def phase2(l):
            with contextlib.ExitStack() as es:
                QT = [sbuf(es, f"QT{i}", [68, T], BF16) for i in range(2)]
                KT = [sbuf(es, f"KT{i}", [68, T], BF16) for i in range(2)]
                VV = [sbuf(es, f"VV{i}", [128, NT, 65], BF16) for i in range(2)]
                Pb = [sbuf(es, f"P{i}", [128, 512], BF16) for i in range(4)]
                mB = sbuf(es, "mB", [128, 20, 512], BF16)
                rden = sbuf(es, "rden", [65, 1024], F32)
                bc = [sbuf(es, f"bc{i}", [64, 512], F32) for i in range(2)]
                tA = [sbuf(es, f"tA{i}", [64, 512], F32) for i in range(3)]
                ost = [sbuf(es, f"ost{i}", [64, 512], BF16) for i in range(2)]

                tr.dma("sp", mB[:], maskB_d.rearrange("k p q -> p k q"), "c0", w=["mB"])
                for i in range(2):
                    tr.op("dve", lambda e, i=i: e.memset(VV[i][:, :, 64:65], 1.0), w=[f"VV{i}"])

                units = [("A", h) for h in range(4)] + [("B", h) for h in range(6)] + [("C", h) for h in range(6)]

                def load_unit(u):
                    kind, h = units[u]
                    s = u % 2
                    if kind == "A":
                        tr.dma("sp", QT[s][0:64, :], qaT[h * 64:(h + 1) * 64, :], f"qkv{s}", w=[f"QT{s}"])
                        tr.dma("sp", KT[s][0:64, :], kaT[h * 64:(h + 1) * 64, :], f"qkv{s}", w=[f"KT{s}"])
                        vsrc = va_d
                    elif kind == "B":
                        tr.dma("sp", QT[s][0:64, :], qbT[h * 64:(h + 1) * 64, :], f"qkv{s}", w=[f"QT{s}"])
                        tr.dma("sp", KT[s][0:64, :], kbT[h * 64:(h + 1) * 64, :], f"qkv{s}", w=[f"KT{s}"])
                        vsrc = vb_d
                    else:
                        tr.dma("sp", QT[s][0:68, :], qcT[h, :, :], f"qkv{s}", w=[f"QT{s}"])
                        tr.dma("sp", KT[s][0:68, :], kcT[h, :, :], f"qkv{s}", w=[f"KT{s}"])
                        vsrc = vc_d
                    tr.dma("sp", VV[s][:, :, 0:64], vsrc[:, h * 64:(h + 1) * 64].rearrange("(n p) d -> p n d", p=128),
                           f"qkv{s}", w=[f"VV{s}"])

                SB = (0, 1, 2)
                OB = (3, 4, 5, 6)
                MISC = 7
                state = dict(n=0, no=0, npost=0)
                pending = []

                def tick():
                    for it in pending:
                        it[0] -= 1
                    while pending and pending[0][0] <= 0:
                        pending.pop(0)[1]()

                def flush():
                    while pending:
                        pending.pop(0)[1]()

                def defer(k, fn):
                    pending.append([k, fn])

                blocks = []

                def emit_pv(bk):
                    (s, i, q0, ob, first, last, pslot) = bk
                    tr.op("pe", lambda e: e.matmul(ps[0:65, ob, q0:512], lhsT=VV[s][:, i, 0:65], rhs=Pb[pslot][:, q0:512],
                                                   start=first, stop=last),
                          r=[f"VV{s}", f"P{pslot}"], w=[f"ps{ob}"])

                def block(s, rows, i, J, q0, mask, scale, ob, first, last):
                    n = state["n"]
                    state["n"] += 1
                    sb_ = SB[n % 3]
                    pslot = n % 4
                    r0, r1_ = rows
                    tr.op("pe", lambda e: e.matmul(ps[:, sb_, q0:512], lhsT=KT[s][r0:r1_, i * 128:(i + 1) * 128],
                                                   rhs=QT[s][r0:r1_, J * 512 + q0:(J + 1) * 512], start=True, stop=True),
                          r=[f"KT{s}", f"QT{s}"], w=[f"ps{sb_}"])
                    tr.op("act", lambda e: e.activation(out=Pb[pslot][:, q0:512], in_=ps[:, sb_, q0:512], func=AF.Exp, scale=scale),
                          r=[f"ps{sb_}"], w=[f"P{pslot}"])
                    if mask is not None:
                        mk, mkey, w_ = mask
                        tr.op("dve", lambda e: e.tensor_tensor(out=Pb[pslot][:, q0:q0 + w_], in0=Pb[pslot][:, q0:q0 + w_],
                                                               in1=mk, op=ALU.mult),
                              r=[f"P{pslot}", mkey], w=[f"P{pslot}"])
                    blocks.append((s, i, q0, ob, first, last, pslot))
                    if len(blocks) > 2:
                        emit_pv(blocks.pop(0))
                    tick()

                def drain_pv():
                    while blocks:
                        emit_pv(blocks.pop(0))

                def post_simple(ob, row0, J):
                    k = state["npost"]
                    state["npost"] += 1
                    o_ = k % 2

                    def s1():
                        tr.op("dve", lambda e: e.reciprocal(out=rden[64:65, 0:512], in_=ps[64:65, ob, :]),
                              r=[f"ps{ob}"], w=["rden"])

                    def s2():
                        tr.op("pe", lambda e: e.matmul(ps[0:64, MISC, :], lhsT=ones32[64:65, 0:64], rhs=rden[64:65, 0:512],
                                                       start=True, stop=True), r=["rden", "ones32"], w=[f"ps{MISC}"])
                        tr.op("dve", lambda e: e.tensor_copy(out=bc[0][:], in_=ps[0:64, MISC, :]), r=[f"ps{MISC}"], w=["bc0"])
                        tr.op("dve", lambda e: e.tensor_tensor(out=ost[o_][:], in0=ps[0:64, ob, :], in1=bc[0][:], op=ALU.mult),
                              r=[f"ps{ob}", "bc0"], w=[f"ost{o_}"])
                        tr.dma("sp", oT_d[row0:row0 + 64, J * 512:(J + 1) * 512], ost[o_][:], f"ost{o_}", r=[f"ost{o_}"])

                    s1()
                    defer(3, s2)

                def post_A(ob0, ob1, row0, J):
                    k = state["npost"]
                    state["npost"] += 1
                    o_ = k % 2

                    def s1():
                        tr.op("dve", lambda e: e.reciprocal(out=rden[64:65, 0:512], in_=ps[64:65, ob0, :]),
                              r=[f"ps{ob0}"], w=["rden"])
                        tr.op("dve", lambda e: e.reciprocal(out=rden[64:65, 512:1024], in_=ps[64:65, ob1, :]),
                              r=[f"ps{ob1}"], w=["rden"])

                    def s2():
                        for m, ob in ((0, ob0), (1, ob1)):
                            tr.op("pe", lambda e, m=m: e.matmul(ps[0:64, MISC, :], lhsT=ones32[64:65, 0:64],
                                                                rhs=rden[64:65, m * 512:(m + 1) * 512], start=True, stop=True),
                                  r=["rden", "ones32"], w=[f"ps{MISC}"])
                            tr.op("dve", lambda e, m=m: e.tensor_copy(out=bc[m][:], in_=ps[0:64, MISC, :]),
                                  r=[f"ps{MISC}"], w=[f"bc{m}"])
                            tr.op("dve", lambda e, m=m, ob=ob: e.tensor_tensor(out=tA[m][:], in0=ps[0:64, ob, :], in1=bc[m][:],
                                                                               op=ALU.mult),
                                  r=[f"ps{ob}", f"bc{m}"], w=[f"tA{m}"])
                        tr.op("dve", lambda e: e.scalar_tensor_tensor(out=tA[2][:], in0=tA[1][:], scalar=sm[0:64, 0:1], in1=tA[0][:],
                                                                      op0=ALU.mult, op1=ALU.add),
                              r=["tA0", "tA1", "sm"], w=["tA2"])
                        tr.op("pool", lambda e: e.tensor_tensor(out=tA[0][:], in0=tA[2][:], in1=tA[2][:], op=ALU.mult),
                              r=["tA2"], w=["tA0"])

                    def s3():
                        tr.op("pe", lambda e: e.matmul(ps[0:64, MISC, :], lhsT=ones32[0:64, 0:64], rhs=tA[0][:],
                                                       start=True, stop=True), r=["tA0", "ones32"], w=[f"ps{MISC}"])
                        tr.op("dve", lambda e: e.tensor_scalar(out=tA[1][:], in0=ps[0:64, MISC, :], scalar1=1.0 / 64, scalar2=SUBLN_EPS,
                                                               op0=ALU.mult, op1=ALU.add), r=[f"ps{MISC}"], w=["tA1"])

                    def s4():
                        tr.op("act", lambda e: e.activation(out=tA[1][:], in_=tA[1][:], func=AF.Ln), r=["tA1"], w=["tA1"])
                        tr.op("act", lambda e: e.activation(out=tA[1][:], in_=tA[1][:], func=AF.Exp, scale=-0.5), r=["tA1"], w=["tA1"])
                        tr.op("dve", lambda e: e.scalar_tensor_tensor(out=ost[o_][:], in0=tA[2][:], scalar=sm[0:64, 1:2], in1=tA[1][:],
                                                                      op0=ALU.mult, op1=ALU.mult),
                              r=["tA2", "tA1", "sm"], w=[f"ost{o_}"])
                        tr.dma("sp", oT_d[row0:row0 + 64, J * 512:(J + 1) * 512], ost[o_][:], f"ost{o_}", r=[f"ost{o_}"])

                    s1()
                    defer(3, s2)
                    defer(6, s3)
                    defer(9, s4)

                load_unit(0)
                for u, (kind, h) in enumerate(units):
                    s = u % 2
                    if u + 1 < len(units):
                        load_unit(u + 1)
                    for J in range(NQ):
                        if kind == "A":
                            obs = (OB[(2 * state["no"]) % 4], OB[(2 * state["no"] + 1) % 4])
                            state["no"] += 1
                            for m in range(2):
                                nk = 4 * J + 4
                                for i in range(nk):
                                    a = i - 4 * J
                                    q0 = 128 * a if a > 0 else 0
                                    mask = (tri[:, :], "tri", 128) if a >= 0 else None
                                    block(s, (32 * m, 32 * m + 32), i, J, q0, mask, 32 ** -0.5, obs[m], i == 0, i == nk - 1)
                            drain_pv()
                            post_A(obs[0], obs[1], h * 64, J)
                        elif kind == "C":
                            ob = OB[(2 * state["no"]) % 4]
                            state["no"] += 1
                            nk = 4 * J + 4
                            for i in range(nk):
                                a = i - 4 * J
                                q0 = 128 * a if a > 0 else 0
                                mask = (tri[:, :], "tri", 128) if a >= 0 else None
                                block(s, (0, 68), i, J, q0, mask, 1.0, ob, i == 0, i == nk - 1)
                            drain_pv()
                            post_simple(ob, 256 + 384 + h * 64, J)
                        else:
                            ob = OB[(2 * state["no"]) % 4]
                            state["no"] += 1
                            i0 = max(0, 4 * J - 16)
                            nk = 4 * J + 4
                            for i in range(i0, nk):
                                a = i - 4 * J
                                q0 = 128 * a if a > 0 else 0
                                di = (4 * J - i) + 3
                                mask = (mB[:, di, q0:512], "mB", 512 - q0)
                                block(s, (0, 64), i, J, q0, mask, 0.125, ob, i == i0, i == nk - 1)
                            drain_pv()
                            post_simple(ob, 256 + h * 64, J)
                flush()
                tr.barrier()

        def phase3(l, x_src):
            with contextlib.ExitStack() as es:
                wout = sbuf(es, "wout", [128, 8, D], BF16)
                wup = sbuf(es, "wup", [128, 8, 2 * DFF], BF16)
                cw = sbuf(es, "cw", [128, NFF, 4], F32)
                oTt = sbuf(es, "oTt", [128, 8, 512], BF16)
                xin = [sbuf(es, f"xin{i}", [128, D], F32) for i in range(2)]
                junk = sbuf(es, "junk", [128, D], BF16)
                ss = sbuf(es, "ss", [128, 4], F32)
                t32 = sbuf(es, "t32", [128, D], F32)
                hb = sbuf(es, "hb", [128, D], BF16)
                hT = sbuf(es, "hT", [128, 8, 512], BF16)
                gbuf = [sbuf(es, f"gbuf{i}", [128, 514], F32) for i in range(2)]
                acc = [sbuf(es, f"acc{i}", [128, 512], F32) for i in range(2)]
                sg = [sbuf(es, f"sg{i}", [128, 512], F32) for i in range(2)]
                hal = sbuf(es, "hal", [128, NFF, 2], F32)
                ast = [sbuf(es, f"ast{i}", [128, 512], BF16) for i in range(4)]

                load_w_bf16(wout, w_out[l].rearrange("(kc p) n -> p kc n", p=128), 8, "wout", "wl")
                load_w_bf16(wup, w_up[l].rearrange("(kc p) n -> p kc n", p=128), 8, "wup", "wl")
                for k in range(3):
                    tr.dma("sp", cw[:, :, k:k + 1], conv_w[l, k:k + 1, :].rearrange("o (f p) -> p f o", p=128), "c0", w=["cw"],
                           allow_slow_non_contiguous=True)
                tr.dma("sp", cw[:, :, 3:4], conv_b[l:l + 1, :].rearrange("o (f p) -> p f o", p=128), "c0", w=["cw"],
                       allow_slow_non_contiguous=True)
                tr.op("dve", lambda e: e.memset(hal[:], 0.0), w=["hal"])

                nx = 0
                for tt in range(NQ):
                    t0 = tt * 512
                    tr.dma("sp", oTt[:], oT_d[:, t0:t0 + 512].rearrange("(kc p) t -> p kc t", p=128), "oTt", w=["oTt"])
                    for s in range(4):
                        xs = nx % 2
                        nx += 1
                        xk = f"xin{xs}"
                        tr.dma("sp", xin[xs][:], x_src[t0 + s * 128:t0 + (s + 1) * 128, :], xk, w=[xk])
                        for j in range(2):
                            b = 2 + j
                            for kc in range(8):
                                tr.op("pe", lambda e, kc=kc, j=j, b=b, s=s: e.matmul(
                                    PS(b), lhsT=oTt[:, kc, s * 128:(s + 1) * 128], rhs=wout[:, kc, j * 512:(j + 1) * 512],
                                    start=(kc == 0), stop=(kc == 7)), r=["oTt", "wout"], w=[f"ps{b}"])
                            tr.op("dve", lambda e, j=j, b=b: e.tensor_tensor(
                                out=t32[:, j * 512:(j + 1) * 512], in0=PS(b), in1=mod[:, 2 * D + j * 512:2 * D + (j + 1) * 512],
                                op=ALU.mult), r=[f"ps{b}", "mod"], w=["t32"])
                        tr.op("pool", lambda e, xs=xs: e.tensor_tensor(out=xin[xs][:], in0=xin[xs][:], in1=t32[:], op=ALU.add),
                              r=["t32", xk], w=[xk])
                        tr.dma("sp", xres[t0 + s * 128:t0 + (s + 1) * 128, :], xin[xs][:], xk, r=[xk])
                        rms_to_hT((junk, ss, t32, hb), xin[xs][:], xk, s, 4 * D, 3 * D, hT, "hT", s % 2, "p3")
                    for f in range(NFF):
                        bu = 4 + (f % 2)
                        bg = 6 + (f % 2)
                        z = f % 2
                        for kc in range(8):
                            tr.op("pe", lambda e, kc=kc, f=f, bu=bu: e.matmul(
                                PS(bu), lhsT=wup[:, kc, f * 128:(f + 1) * 128], rhs=hT[:, kc, :], start=(kc == 0), stop=(kc == 7)),
                                r=["wup", "hT"], w=[f"ps{bu}"])
                        for kc in range(8):
                            tr.op("pe", lambda e, kc=kc, f=f, bg=bg: e.matmul(
                                PS(bg), lhsT=wup[:, kc, DFF + f * 128:DFF + (f + 1) * 128], rhs=hT[:, kc, :],
                                start=(kc == 0), stop=(kc == 7)), r=["wup", "hT"], w=[f"ps{bg}"])
                        gk = f"gbuf{z}"
                        tr.op("pool", lambda e, z=z, f=f: e.tensor_copy(out=gbuf[z][:, 0:2], in_=hal[:, f, :]), r=["hal"], w=[gk])
                        tr.op("act", lambda e, z=z, bg=bg: e.activation(out=gbuf[z][:, 2:514], in_=PS(bg), func=AF.Copy),
                              r=[f"ps{bg}"], w=[gk])
                        tr.op("pool", lambda e, z=z, f=f: e.tensor_copy(out=hal[:, f, :], in_=gbuf[z][:, 512:514]), r=[gk], w=["hal"])
                        ak = f"acc{z}"
                        tr.op("dve", lambda e, z=z, f=f: e.tensor_scalar(
                            out=acc[z][:], in0=gbuf[z][:, 2:514], scalar1=cw[:, f, 2:3], scalar2=cw[:, f, 3:4],
                            op0=ALU.mult, op1=ALU.add), r=[gk, "cw"], w=[ak])
                        tr.op("dve", lambda e, z=z, f=f: e.scalar_tensor_tensor(
                            out=acc[z][:], in0=gbuf[z][:, 1:513], scalar=cw[:, f, 1:2], in1=acc[z][:], op0=ALU.mult, op1=ALU.add),
                            r=[gk, "cw", ak], w=[ak])
                        tr.op("dve", lambda e, z=z, f=f: e.scalar_tensor_tensor(
                            out=acc[z][:], in0=gbuf[z][:, 0:512], scalar=cw[:, f, 0:1], in1=acc[z][:], op0=ALU.mult, op1=ALU.add),
                            r=[gk, "cw", ak], w=[ak])
                        sk = f"sg{z}"
                        tr.op("act", lambda e, z=z: e.activation(out=sg[z][:], in_=acc[z][:], func=AF.Silu), r=[ak], w=[sk])
                        a4 = f % 4
                        tr.op("dve", lambda e, z=z, a4=a4, bu=bu: e.tensor_tensor(out=ast[a4][:], in0=PS(bu), in1=sg[z][:], op=ALU.mult),
                              r=[f"ps{bu}", sk], w=[f"ast{a4}"])
                        tr.dma("sp", aT_d[f * 128:(f + 1) * 128, t0:t0 + 512], ast[a4][:], f"ast{a4}", r=[f"ast{a4}"])
                tr.barrier()

        def phase4(l, dst, do_final):
            with contextlib.ExitStack() as es:
                wdn = sbuf(es, "wdn", [128, NFF, D], BF16)
                aTt = [sbuf(es, f"aTt{i}", [128, NFF, 512], BF16) for i in range(2)]
                xin = [sbuf(es, f"xin{i}", [128, D], F32) for i in range(3)]
                t32 = sbuf(es, "t32", [128, D], F32)
                junk = sbuf(es, "junk", [128, D], BF16)
                ss = sbuf(es, "ss", [128, 4], F32)
                gf = sbuf(es, "gf", [128, D], F32)
                load_w_bf16(wdn, w_down[l].rearrange("(f p) n -> p f n", p=128), NFF, "wdn", "wl")
                if do_final:
                    tr.dma("sp", gf[:], g_fin[0:1, :].partition_broadcast(128), "c0", w=["gf"])
                nx = 0
                for tt in range(NQ):
                    t0 = tt * 512
                    a_ = tt % 2
                    akey = f"aTt{a_}"
                    tr.dma("sp", aTt[a_][:], aT_d[:, t0:t0 + 512].rearrange("(f p) t -> p f t", p=128), akey, w=[akey])
                    for s in range(4):
                        xs = nx % 3
                        nx += 1
                        xk = f"xin{xs}"
                        tr.dma("sp", xin[xs][:], xres[t0 + s * 128:t0 + (s + 1) * 128, :], xk, w=[xk])
                        for j in range(2):
                            b = (2 * s + j) % 4
                            for f in range(NFF):
                                tr.op("pe", lambda e, f=f, j=j, b=b, s=s, a_=a_: e.matmul(
                                    PS(b), lhsT=aTt[a_][:, f, s * 128:(s + 1) * 128], rhs=wdn[:, f, j * 512:(j + 1) * 512],
                                    start=(f == 0), stop=(f == NFF - 1)), r=[akey, "wdn"], w=[f"ps{b}"])
                            tr.op("dve", lambda e, j=j, b=b: e.tensor_tensor(
                                out=t32[:, j * 512:(j + 1) * 512], in0=PS(b), in1=mod[:, 5 * D + j * 512:5 * D + (j + 1) * 512],
                                op=ALU.mult), r=[f"ps{b}", "mod"], w=["t32"])
                        tr.op("pool", lambda e, xs=xs: e.tensor_tensor(out=xin[xs][:], in0=xin[xs][:], in1=t32[:], op=ALU.add),
                              r=["t32", xk], w=[xk])
                        if do_final:
                            tr.op("act", lambda e, xs=xs: e.activation(out=junk[:], in_=xin[xs][:], func=AF.Square, accum_out=ss[:, 0:1]),
                                  r=[xk], w=["junk", "ss"])
                            tr.op("dve", lambda e: e.tensor_scalar(out=ss[:, 1:2], in0=ss[:, 0:1], scalar1=1.0 / D, scalar2=NORM_EPS,
                                                                   op0=ALU.mult, op1=ALU.add), r=["ss"], w=["ss"])
                            tr.op("act", lambda e: e.activation(out=ss[:, 2:3], in_=ss[:, 1:2], func=AF.Sqrt), r=["ss"], w=["ss"])
                            tr.op("dve", lambda e: e.reciprocal(out=ss[:, 3:4], in_=ss[:, 2:3]), r=["ss"], w=["ss"])
                            tr.op("dve", lambda e, xs=xs: e.scalar_tensor_tensor(
                                out=xin[xs][:], in0=xin[xs][:], scalar=ss[:, 3:4], in1=gf[:], op0=ALU.mult, op1=ALU.mult),
                                r=[xk, "ss", "gf"], w=[xk])
                        tr.dma("sp", dst[t0 + s * 128:t0 + (s + 1) * 128, :], xin[xs][:], xk, r=[xk])
                tr.barrier()

        for l in range(L):
            phase0(l)
            phase1(l, x_d if l == 0 else xres)
            phase2(l)
            phase3(l, x_d if l == 0 else xres)
            last = (l == L - 1)
            phase4(l, y_d if last else xres, last and final)
        tr.barrier()
        print("kernel instructions (approx):", tr.ninst, flush=True)
    return nc


def _rope_tab(T, dim, theta=10000.0):
    inv = (1.0 / (np.float32(theta) ** (np.arange(0, dim, 2, dtype=np.float32) / np.float32(dim)))).astype(np.float32)
    ang = (np.arange(T, dtype=np.float32)[:, None] * inv[None, :]).astype(np.float32)
    cos = np.cos(ang).astype(np.float32).T
    sin = np.sin(ang).astype(np.float32).T
    tab = np.empty((2, dim, T), np.float32)
    tab[0] = np.concatenate([cos, cos], 0)
    tab[1] = np.concatenate([-sin, sin], 0)
    return tab


def _consts(T):
    kk = np.arange(128)[:, None]
    qq = np.arange(512)[None, :]
    mB = np.zeros((20, 128, 512), np.float32)
    for di in range(20):
        dist = (di - 3) * 128 + qq - kk
        m = ((dist >= 0) & (dist <= 128)).astype(np.float32)
        m += ((dist >= 0) & (dist <= 512) & (dist % 4 == 0)).astype(np.float32)
        m += ((dist >= 0) & (dist <= 2048) & (dist % 16 == 0)).astype(np.float32)
        mB[di] = m
    tri = (np.arange(128)[:, None] <= np.arange(128)[None, :]).astype(np.float32)
    return dict(
        ropeA=_rope_tab(T, 32), ropeB=_rope_tab(T, 64),
        maskB=mB.astype(ml_dtypes.bfloat16), tri=tri.astype(ml_dtypes.bfloat16),
        identb=np.eye(128, dtype=np.float32).astype(ml_dtypes.bfloat16),
    )


def _rot_perm():
    idx = []
    for base in (QA0, KA0):
        for u in range(8):
            for d in range(32):
                idx.append(base + u * 32 + (d + 16) % 32)
    for base in (QB0, KB0):
        for h in range(6):
            for d in range(64):
                idx.append(base + h * 64 + (d + 32) % 64)
    return np.asarray(idx, np.int64)


def make_in_maps(inputs, T, layers, ncores):
    f = lambda a: np.ascontiguousarray(np.asarray(a, dtype=np.float32))
    L = len(layers)
    ls = list(layers)
    perm = _rot_perm()
    w_in = f(inputs["w_in"])[ls]
    shared = dict(
        w_mod=f(inputs["w_mod"])[ls], b_mod=f(inputs["b_mod"])[ls], g_attn=f(inputs["g_attn"])[ls],
        w_in=np.ascontiguousarray(w_in), w_inr=np.ascontiguousarray(w_in[:, :, perm]),
        diff_lambda=f(inputs["diff_lambda"])[ls].reshape(L, 128), subln_g=f(inputs["subln_g"])[ls],
        forget_bias=f(inputs["forget_bias"])[ls], w_out=f(inputs["w_out"])[ls], g_mlp=f(inputs["g_mlp"])[ls],
        w_up=f(inputs["w_up"])[ls], conv_w=f(inputs["conv_w"])[ls], conv_b=f(inputs["conv_b"])[ls],
        w_down=f(inputs["w_down"])[ls], g_final=f(inputs["g_final"]).reshape(1, D),
        lamc=np.asarray([[0.8 - 0.6 * math.exp(-0.3 * l), 1.0 - (0.8 - 0.6 * math.exp(-0.3 * l))] for l in ls], np.float32),
    )
    shared.update(_consts(T))
    x = np.asarray(inputs["x"], dtype=np.float32)
    c = np.asarray(inputs["c"], dtype=np.float32)
    maps = []
    for b in range(ncores):
        m = dict(shared)
        m["x"] = np.ascontiguousarray(x[b, :T])
        m["c"] = np.ascontiguousarray(c[b])
        maps.append(m)
    return maps


def kernel(x, c, w_mod, b_mod, g_attn, w_in, diff_lambda, subln_g, forget_bias, w_out,
           g_mlp, w_up, conv_w, conv_b, w_down, g_final):
    inputs = dict(x=x, c=c, w_mod=w_mod, b_mod=b_mod, g_attn=g_attn, w_in=w_in, diff_lambda=diff_lambda,
                  subln_g=subln_g, forget_bias=forget_bias, w_out=w_out, g_mlp=g_mlp, w_up=w_up,
                  conv_w=conv_w, conv_b=conv_b, w_down=w_down, g_final=g_final)
    B, T, _ = np.asarray(x).shape
    L = np.asarray(w_in).shape[0]
    assert B == NCORES
    nc = build(T, L, final=True)
    maps = make_in_maps(inputs, T, range(L), NCORES)
    res = run_bass_kernel_spmd(nc, maps, core_ids=list(range(NCORES)))
    return np.stack([np.asarray(r["y"], dtype=np.float32) for r in res.results], axis=0)
```

```python
import contextlib
import math

import ml_dtypes
import numpy as np

import concourse.bass as bass
import concourse.mybir as mybir
from concourse.bass_utils import run_bass_kernel_spmd

F32 = mybir.dt.float32
BF16 = mybir.dt.bfloat16
AF = mybir.ActivationFunctionType
ALU = mybir.AluOpType
AX = mybir.AxisListType

D = 1024
DFF = 2816
NFF = 22
INC = 3078
ROTC = 1280
NORM_EPS = 1e-6
SUBLN_EPS = 1e-5
NCORES = 8
SAME_ENGINE_SYNC = True

QA0, KA0, VA0, QB0, KB0, VB0, QC0, KC0, VC0, FZ0 = 0, 256, 512, 768, 1152, 1536, 1920, 2304, 2688, 3072


class Tracker:
    def __init__(self, nc, es):
        self.nc = nc
        self.E = dict(pe=nc.tensor, act=nc.scalar, dve=nc.vector, pool=nc.gpsimd, sp=nc.sync)
        self.sem = {}
        self.cnt = {}
        self.es = es
        for e in ("pe", "act", "dve", "pool"):
            self.sem[e] = es.enter_context(nc.semaphore("prog_" + e))
            self.cnt[e] = 0
        self.waited = {e: {} for e in self.E}
        self.clock = {}
        self.bw = {}
        self.br = {}
        self.ninst = 0

    def dsem(self, name):
        if name not in self.sem:
            self.sem[name] = self.es.enter_context(self.nc.semaphore("d_" + name))
            self.cnt[name] = 0
        return name

    def _deps(self, r, w):
        d = {}
        raw = {}

        def add(k, v, is_raw=False):
            if k not in self.E:
                v = self.cnt[k]
            if d.get(k, 0) < v:
                d[k] = v
            if is_raw and raw.get(k, 0) < v:
                raw[k] = v

        for b in r:
            t = self.bw.get(b)
            if t:
                add(t[0], t[1], True)
        for b in w:
            t = self.bw.get(b)
            if t:
                add(*t)
            for k, v in self.br.get(b, {}).items():
                add(k, v)
        self._raw = raw
        return d

    def _waits(self, e, d):
        order = sorted(d.items(), key=lambda kv: (0 if kv[0] in ("pe", "dve", "act", "pool") else 1))
        for k, v in order:
            if k == e:
                if e == "pe" or not SAME_ENGINE_SYNC:
                    continue
                v = self._raw.get(k, 0)
                if v == 0:
                    continue
            if self.waited[e].get(k, 0) >= v:
                continue
            self.E[e].wait_ge(self.sem[k], v)
            self.ninst += 1
            self._learn(e, k, v)

    def _learn(self, e, k, v):
        we = self.waited[e]
        we[k] = v
        for kk, vv in self.clock.get((k, v), {}).items():
            if kk != e and we.get(kk, 0) < vv:
                we[kk] = vv

    def _record(self, tok, r, w):
        k, v = tok
        for b in r:
            self.br.setdefault(b, {})[k] = v
        for b in w:
            self.bw[b] = tok
            self.br[b] = {}

    def op(self, e, fn, r=(), w=()):
        self._waits(e, self._deps(r, w))
        inst = fn(self.E[e])
        self.cnt[e] += 1
        inst.then_inc(self.sem[e], 1)
        self.ninst += 1
        self.clock[(e, self.cnt[e])] = dict(self.waited[e])
        self._record((e, self.cnt[e]), r, w)

    def dma(self, q, out, in_, sem, r=(), w=(), **kw):
        self.dsem(sem)
        self._waits(q, self._deps(r, w))
        inst = self.E[q].dma_start(out=out, in_=in_, **kw)
        self.cnt[sem] += 16
        inst.then_inc(self.sem[sem], 16)
        self.ninst += 1
        self.clock[(sem, self.cnt[sem])] = dict(self.waited[q])
        self._record((sem, self.cnt[sem]), r, w)

    def barrier(self, engines=("pe", "act", "dve", "pool", "sp")):
        for e in engines:
            for k, v in self.cnt.items():
                if v == 0 or k == e:
                    continue
                if self.waited[e].get(k, 0) >= v:
                    continue
                self.E[e].wait_ge(self.sem[k], v)
                self.waited[e][k] = v
        self.clock = {}
        self.bw = {}
        self.br = {}


def build(T, L, final=True, dbg=False, emit_x=False):
    NQ = T // 512
    NT = T // 128
    nc = bass.Bass("TRN2", target_bir_lowering=False)

    def din(name, shape, dt=F32):
        return nc.dram_tensor(name, list(shape), dt, kind="ExternalInput").ap()

    def dscr(name, shape, dt):
        kind = "ExternalOutput" if dbg else "Internal"
        return nc.dram_tensor(name, list(shape), dt, kind=kind).ap()

    x_d = din("x", [T, D])
    c_d = din("c", [D])
    w_mod = din("w_mod", [L, D, 6 * D])
    b_mod = din("b_mod", [L, 6 * D])
    g_attn = din("g_attn", [L, D])
    w_in = din("w_in", [L, D, INC])
    w_inr = din("w_inr", [L, D, ROTC])
    dlam = din("diff_lambda", [L, 128])
    subg = din("subln_g", [L, 64])
    fbias = din("forget_bias", [L, 6])
    w_out = din("w_out", [L, D, D])
    g_mlp = din("g_mlp", [L, D])
    w_up = din("w_up", [L, D, 2 * DFF])
    conv_w = din("conv_w", [L, 3, DFF])
    conv_b = din("conv_b", [L, DFF])
    w_down = din("w_down", [L, DFF, D])
    g_fin = din("g_final", [1, D])
    lamc = din("lamc", [L, 2])
    ropeA = din("ropeA", [2, 32, T])
    ropeB = din("ropeB", [2, 64, T])
    maskB_d = din("maskB", [20, 128, 512], BF16)
    tri_d = din("tri", [128, 128], BF16)
    ident_d = din("identb", [128, 128], BF16)

    y_d = nc.dram_tensor("y", [T, D], F32, kind="ExternalOutput").ap()
    xo_d = nc.dram_tensor("xo", [T, D], F32, kind="ExternalOutput").ap() if emit_x else None

    xres = dscr("xres", [T, D], F32)
    qaT = dscr("qaT", [256, T], BF16)
    kaT = dscr("kaT", [256, T], BF16)
    qbT = dscr("qbT", [384, T], BF16)
    kbT = dscr("kbT", [384, T], BF16)
    qcT = dscr("qcT", [6, 68, T], BF16)
    kcT = dscr("kcT", [6, 68, T], BF16)
    va_d = dscr("va", [T, 256], BF16)
    vb_d = dscr("vb", [T, 384], BF16)
    vc_d = dscr("vc", [T, 384], BF16)
    oT_d = dscr("oT", [D, T], BF16)
    aT_d = dscr("aT", [DFF, T], BF16)

    top = contextlib.ExitStack()
    with top:
        tr = Tracker(nc, top)

        uniq = [0]

        def sbuf(es, name, shape, dt):
            uniq[0] += 1
            return es.enter_context(nc.sbuf_tensor(f"{name}_u{uniq[0]}", list(shape), dt))

        ps = top.enter_context(nc.psum_tensor("ps", [128, 8, 512], F32))

        def PS(b):
            return ps[:, b, :]

        def PSB(b):
            return ps[:, b, :].bitcast(BF16)

        mod = sbuf(top, "mod", [128, 6 * D], F32)
        SC = sbuf(top, "SC", [128, 8, 128], F32)
        identb = sbuf(top, "identb_s", [128, 128], BF16)
        tri = sbuf(top, "tri_s", [128, 128], BF16)
        ones32 = sbuf(top, "ones32", [128, 128], F32)
        onesb = sbuf(top, "onesb", [128, 512], BF16)
        sm = sbuf(top, "sm", [128, 16], F32)
        lamt = sbuf(top, "lamt", [128, 2], F32)

        tr.dma("sp", identb[:], ident_d[:, :], "c0", w=["identb"])
        tr.dma("sp", tri[:], tri_d[:, :], "c0", w=["tri"])
        tr.op("dve", lambda e: e.memset(ones32[:], 1.0), w=["ones32"])
        tr.op("dve", lambda e: e.memset(onesb[:], 1.0), w=["onesb"])

        with contextlib.ExitStack() as es:
            ct = sbuf(es, "ct", [128, 8], F32)
            sct = sbuf(es, "sct", [128, 8], F32)
            tr.dma("sp", ct[:], c_d.rearrange("(kc p) -> p kc", p=128), "c0", w=["ct"],
                   allow_slow_non_contiguous=True)
            tr.op("act", lambda e: e.activation(out=sct[:], in_=ct[:], func=AF.Silu), r=["ct"], w=["sct"])
            for kc in range(8):
                tr.op("dve", lambda e, kc=kc: e.tensor_scalar(
                    out=SC[:, kc, :], in0=ones32[:, :], scalar1=sct[:, kc:kc + 1], scalar2=None, op0=ALU.mult),
                    r=["sct", "ones32"], w=["SC"])
            tr.barrier()

        def phase0(l):
            with contextlib.ExitStack() as es:
                wm = [sbuf(es, f"wm{i}", [128, 8, 512], F32) for i in range(2)]
                bm = [sbuf(es, f"bm{i}", [128, 512], F32) for i in range(2)]
                gb = sbuf(es, "gb", [128, D], F32)
                dl = sbuf(es, "dl", [64, 128], F32)
                tmp = sbuf(es, "tmp0", [64, 64], F32)
                wv = w_mod[l].rearrange("(kc p) n -> p kc n", p=128)
                for j in range(12):
                    s = j % 2
                    tr.dma("sp", wm[s][:], wv[:, :, j * 512:(j + 1) * 512], f"wm{s}", w=[f"wm{s}"])
                    tr.dma("pool", bm[s][:], b_mod[l:l + 1, j * 512:(j + 1) * 512].partition_broadcast(128),
                           f"bm{s}", w=[f"bm{s}"])
                    b = j % 2
                    for kc in range(8):
                        tr.op("pe", lambda e, kc=kc, s=s, b=b: e.matmul(
                            PS(b), lhsT=SC[:, kc, :], rhs=wm[s][:, kc, :], start=(kc == 0), stop=(kc == 7)),
                            r=[f"wm{s}", "SC"], w=[f"ps{b}"])
                    tr.op("dve", lambda e, s=s, b=b, j=j: e.tensor_tensor(
                        out=mod[:, j * 512:(j + 1) * 512], in0=PS(b), in1=bm[s][:], op=ALU.add),
                        r=[f"ps{b}", f"bm{s}"], w=["mod"])
                for (gsrc, off) in ((g_attn, D), (g_mlp, 4 * D)):
                    tr.dma("sp", gb[:], gsrc[l:l + 1, :].partition_broadcast(128), "c0", w=["gb"])
                    tr.op("dve", lambda e, off=off: e.scalar_tensor_tensor(
                        out=mod[:, off:off + D], in0=mod[:, off:off + D], scalar=1.0, in1=gb[:],
                        op0=ALU.add, op1=ALU.mult), r=["gb", "mod"], w=["mod"])
                tr.dma("sp", dl[:], dlam[l:l + 1, :].partition_broadcast(64), "c0", w=["dl"])
                tr.dma("sp", lamt[0:64, :], lamc[l:l + 1, :].partition_broadcast(64), "c0", w=["lamt"])
                tr.dma("sp", sm[0:64, 1:2], subg[l:l + 1, :].rearrange("o d -> d o"), "c0", w=["sm"])
                tr.dma("sp", sm[0:6, 2:3], fbias[l:l + 1, :].rearrange("o d -> d o"), "c0", w=["sm"])
                tr.op("dve", lambda e: e.tensor_tensor(out=tmp[:, 0:32], in0=dl[:, 0:32], in1=dl[:, 32:64], op=ALU.mult),
                      r=["dl"], w=["tmp0"])
                tr.op("dve", lambda e: e.tensor_tensor(out=tmp[:, 32:64], in0=dl[:, 64:96], in1=dl[:, 96:128], op=ALU.mult),
                      r=["dl", "tmp0"], w=["tmp0"])
                tr.op("dve", lambda e: e.tensor_reduce(out=sm[0:64, 3:4], in_=tmp[:, 0:32], axis=AX.X, op=ALU.add),
                      r=["tmp0", "sm"], w=["sm"])
                tr.op("dve", lambda e: e.tensor_reduce(out=sm[0:64, 4:5], in_=tmp[:, 32:64], axis=AX.X, op=ALU.add),
                      r=["tmp0", "sm"], w=["sm"])
                tr.op("act", lambda e: e.activation(out=sm[0:64, 5:7], in_=sm[0:64, 3:5], func=AF.Exp), r=["sm"], w=["sm"])
                tr.op("dve", lambda e: e.tensor_tensor(out=sm[0:64, 7:8], in0=sm[0:64, 6:7], in1=sm[0:64, 5:6], op=ALU.subtract),
                      r=["sm"], w=["sm"])
                tr.op("dve", lambda e: e.tensor_tensor(out=sm[0:64, 0:1], in0=sm[0:64, 7:8], in1=lamt[0:64, 0:1], op=ALU.subtract),
                      r=["sm", "lamt"], w=["sm"])
                tr.op("dve", lambda e: e.tensor_tensor(out=sm[0:64, 1:2], in0=sm[0:64, 1:2], in1=lamt[0:64, 1:2], op=ALU.mult),
                      r=["sm", "lamt"], w=["sm"])
                tr.op("dve", lambda e: e.tensor_scalar(out=sm[0:6, 2:3], in0=sm[0:6, 2:3], scalar1=-1.0, scalar2=None, op0=ALU.mult),
                      r=["sm"], w=["sm"])
                tr.barrier()

        def rms_to_hT(es_bufs, xin_t, xkey, s, gmod_off, shift_off, hT, hTkey, psb, tag):
            junk, ss, t32, hb = es_bufs
            tr.op("act", lambda e: e.activation(out=junk[:], in_=xin_t, func=AF.Square, accum_out=ss[:, 0:1]),
                  r=[xkey], w=["junk", "ss"])
            tr.op("dve", lambda e: e.tensor_scalar(out=ss[:, 1:2], in0=ss[:, 0:1], scalar1=1.0 / D, scalar2=NORM_EPS,
                                                   op0=ALU.mult, op1=ALU.add), r=["ss"], w=["ss"])
            tr.op("act", lambda e: e.activation(out=ss[:, 2:3], in_=ss[:, 1:2], func=AF.Sqrt), r=["ss"], w=["ss"])
            tr.op("dve", lambda e: e.reciprocal(out=ss[:, 3:4], in_=ss[:, 2:3]), r=["ss"], w=["ss"])
            tr.op("dve", lambda e: e.scalar_tensor_tensor(out=t32[:], in0=xin_t, scalar=ss[:, 3:4],
                                                          in1=mod[:, gmod_off:gmod_off + D], op0=ALU.mult, op1=ALU.mult),
                  r=[xkey, "ss", "mod"], w=["t32"])
            tr.op("pool", lambda e: e.tensor_tensor(out=hb[:], in0=t32[:], in1=mod[:, shift_off:shift_off + D], op=ALU.add),
                  r=["t32", "mod"], w=["hb"])
            for kc in range(8):
                tr.op("pe", lambda e, kc=kc: e.transpose(out=PSB(psb)[:, kc * 128:(kc + 1) * 128],
                                                         in_=hb[:, kc * 128:(kc + 1) * 128], identity=identb[:]),
                      r=["hb", "identb"], w=[f"ps{psb}"])
            tr.op("act", lambda e: e.activation(out=hT[:, :, s * 128:(s + 1) * 128],
                                                in_=PSB(psb).rearrange("p (kc t) -> p kc t", kc=8), func=AF.Copy),
                  r=[f"ps{psb}"], w=[hTkey])

        def load_w_bf16(dst, src_view, nk, key, sem):
            for kc in range(nk):
                tr.dma("pool", dst[:, kc, :], src_view[:, kc, :], sem, w=[key])

        def phase1(l, x_src):
            with contextlib.ExitStack() as es:
                win = sbuf(es, "win", [128, 8, INC], BF16)
                winr = sbuf(es, "winr", [128, 8, ROTC], BF16)
                xin = [sbuf(es, f"xin{i}", [128, D], F32) for i in range(3)]
                junk = sbuf(es, "junk", [128, D], BF16)
                ss = sbuf(es, "ss", [128, 4], F32)
                t32 = sbuf(es, "t32", [128, D], F32)
                hb = sbuf(es, "hb", [128, D], BF16)
                hT = [sbuf(es, f"hT{i}", [128, 8, 512], BF16) for i in range(2)]
                tabA = sbuf(es, "tabA", [128, 2, 512], F32)
                tabB = sbuf(es, "tabB", [128, 2, 512], F32)
                r1 = [sbuf(es, f"r1_{i}", [128, 512], F32) for i in range(2)]
                r2 = [sbuf(es, f"r2_{i}", [128, 512], F32) for i in range(2)]
                qst = [sbuf(es, f"qst{i}", [128, 512], BF16) for i in range(4)]
                vst = sbuf(es, "vst", [128, 4, 1024], BF16)
                fzt = sbuf(es, "fzt", [6, 8, 512], F32)
                fzb = sbuf(es, "fzb", [6, 4, 512], BF16)
                carry = sbuf(es, "carry", [6, 1], F32)

                load_w_bf16(win, w_in[l].rearrange("(kc p) n -> p kc n", p=128), 8, "win", "wl")
                load_w_bf16(winr, w_inr[l].rearrange("(kc p) n -> p kc n", p=128), 8, "winr", "wl")
                tr.op("dve", lambda e: e.memset(carry[:], 0.0), w=["carry"])
                tr.op("dve", lambda e: e.memset(fzt[:, 3, :], 1.0), w=["fzt3"])

                nx = 0
                nq = 0
                chunks = []
                for j in range(2):
                    chunks.append((QA0 + j * 128, j * 128, "A", qaT, j))
                for j in range(2):
                    chunks.append((KA0 + j * 128, 256 + j * 128, "A", kaT, j))
                for j in range(3):
                    chunks.append((QB0 + j * 128, 512 + j * 128, "B", qbT, j))
                for j in range(3):
                    chunks.append((KB0 + j * 128, 896 + j * 128, "B", kbT, j))
                for j in range(3):
                    chunks.append((QC0 + j * 128, None, "Cq", qcT, j))
                for j in range(3):
                    chunks.append((KC0 + j * 128, None, "Ck", kcT, j))

                for tt in range(NQ):
                    t0 = tt * 512
                    hs = tt % 2
                    hTt = hT[hs]
                    hkey = f"hT{hs}"
                    for s in range(4):
                        xs = nx % 3
                        nx += 1
                        tr.dma("sp", xin[xs][:], x_src[t0 + s * 128:t0 + (s + 1) * 128, :], f"xin{xs}", w=[f"xin{xs}"])
                        rms_to_hT((junk, ss, t32, hb), xin[xs][:], f"xin{xs}", s, D, 0, hTt, hkey, s % 2, "p1")
                    for u in range(4):
                        tr.dma("sp", tabA[u * 32:(u + 1) * 32, :, :],
                               ropeA[:, :, t0:t0 + 512].rearrange("c d t -> d c t"), "tab", w=["tabA"])
                    for u in range(2):
                        tr.dma("sp", tabB[u * 64:(u + 1) * 64, :, :],
                               ropeB[:, :, t0:t0 + 512].rearrange("c d t -> d c t"), "tab", w=["tabB"])
                    for ci, (c0, rc0, kind, dst, j) in enumerate(chunks):
                        b = 2 + (ci % 2)
                        for kc in range(8):
                            tr.op("pe", lambda e, kc=kc, c0=c0, b=b: e.matmul(
                                PS(b), lhsT=win[:, kc, c0:c0 + 128], rhs=hTt[:, kc, :], start=(kc == 0), stop=(kc == 7)),
                                r=["win", hkey], w=[f"ps{b}"])
                        qs = nq % 4
                        nq += 1
                        if rc0 is not None:
                            br_ = 4 + (ci % 2)
                            for kc in range(8):
                                tr.op("pe", lambda e, kc=kc, rc0=rc0, br_=br_: e.matmul(
                                    PS(br_), lhsT=winr[:, kc, rc0:rc0 + 128], rhs=hTt[:, kc, :], start=(kc == 0), stop=(kc == 7)),
                                    r=["winr", hkey], w=[f"ps{br_}"])
                            tab = tabA if kind == "A" else tabB
                            tkey = "tabA" if kind == "A" else "tabB"
                            rs = ci % 2
                            tr.op("dve", lambda e, b=b, rs=rs, tab=tab: e.tensor_tensor(
                                out=r1[rs][:], in0=PS(b), in1=tab[:, 0, :], op=ALU.mult),
                                r=[f"ps{b}", tkey], w=[f"r1_{rs}"])
                            tr.op("dve", lambda e, br_=br_, rs=rs, tab=tab: e.tensor_tensor(
                                out=r2[rs][:], in0=PS(br_), in1=tab[:, 1, :], op=ALU.mult),
                                r=[f"ps{br_}", tkey], w=[f"r2_{rs}"])
                            tr.op("pool", lambda e, rs=rs, qs=qs: e.tensor_tensor(
                                out=qst[qs][:], in0=r1[rs][:], in1=r2[rs][:], op=ALU.add),
                                r=[f"r1_{rs}", f"r2_{rs}"], w=[f"qst{qs}"])
                            tr.dma("sp", dst[j * 128:(j + 1) * 128, t0:t0 + 512], qst[qs][:], f"qst{qs}", r=[f"qst{qs}"])
                        else:
                            sc = 0.125 if kind == "Cq" else 1.0
                            tr.op("act", lambda e, b=b, qs=qs, sc=sc: e.activation(
                                out=qst[qs][:], in_=PS(b), func=AF.Copy, scale=sc),
                                r=[f"ps{b}"], w=[f"qst{qs}"])
                            for hh in range(2):
                                tr.dma("sp", dst[2 * j + hh, 0:64, t0:t0 + 512], qst[qs][hh * 64:(hh + 1) * 64, :],
                                       f"qst{qs}", r=[f"qst{qs}"])
                    for s in range(4):
                        for gi, (c0, n, o0) in enumerate(((VA0, 256, 0), (VB0, 384, 256), (VC0, 384, 640))):
                            b = 6 + ((s * 3 + gi) % 2)
                            for kc in range(8):
                                tr.op("pe", lambda e, kc=kc, c0=c0, n=n, b=b, s=s: e.matmul(
                                    PS(b)[:, 0:n], lhsT=hTt[:, kc, s * 128:(s + 1) * 128], rhs=win[:, kc, c0:c0 + n],
                                    start=(kc == 0), stop=(kc == 7)), r=["win", hkey], w=[f"ps{b}"])
                            tr.op("act", lambda e, b=b, n=n, o0=o0, s=s: e.activation(
                                out=vst[:, s, o0:o0 + n], in_=PS(b)[:, 0:n], func=AF.Copy), r=[f"ps{b}"], w=["vst"])
                    for (dst, n, o0) in ((va_d, 256, 0), (vb_d, 384, 256), (vc_d, 384, 640)):
                        tr.dma("sp", dst[t0:t0 + 512, :].rearrange("(s p) c -> p s c", p=128), vst[:, :, o0:o0 + n],
                               "vst", r=["vst"])
                    b = 6
                    for kc in range(8):
                        tr.op("pe", lambda e, kc=kc: e.matmul(
                            PS(6)[0:6, :], lhsT=win[:, kc, FZ0:FZ0 + 6], rhs=hTt[:, kc, :], start=(kc == 0), stop=(kc == 7)),
                            r=["win", hkey], w=["ps6"])
                    tr.op("act", lambda e: e.activation(out=fzt[:, 0, :], in_=PS(6)[0:6, :], func=AF.Exp,
                                                        scale=-1.0, bias=sm[0:6, 2:3]), r=["ps6", "sm"], w=["fzt0"])
                    tr.op("act", lambda e: e.activation(out=fzt[:, 1, :], in_=fzt[:, 0, :], func=AF.Ln, bias=1.0),
                          r=["fzt0"], w=["fzt1"])
                    tr.op("dve", lambda e: e.tensor_tensor_scan(out=fzt[:, 2, :], data0=fzt[:, 3, :], data1=fzt[:, 1, :],
                                                                initial=carry[:, 0:1], op0=ALU.mult, op1=ALU.subtract),
                          r=["fzt1", "fzt3", "carry", "fzb"], w=["fzt2"])
                    tr.op("dve", lambda e: e.tensor_copy(out=carry[:], in_=fzt[:, 2, 511:512]), r=["fzt2"], w=["carry"])
                    tr.op("dve", lambda e: e.tensor_copy(out=fzb[:, 0, :], in_=fzt[:, 2, :]), r=["fzt2"], w=["fzb"])
                    tr.op("dve", lambda e: e.tensor_tensor(out=fzb[:, 1, :], in0=fzt[:, 2, :], in1=fzb[:, 0, :], op=ALU.subtract),
                          r=["fzt2", "fzb"], w=["fzb"])
                    tr.op("dve", lambda e: e.tensor_scalar(out=fzb[:, 2:4, :], in0=fzb[:, 0:2, :], scalar1=-1.0, scalar2=None,
                                                           op0=ALU.mult), r=["fzb"], w=["fzb"])
                    for (dst, row, src) in ((qcT, 64, fzb[:, 0, :]), (qcT, 65, fzb[:, 1, :]), (qcT, 66, onesb[0:6, :]),
                                            (qcT, 67, onesb[0:6, :]), (kcT, 64, onesb[0:6, :]), (kcT, 65, onesb[0:6, :]),
                                            (kcT, 66, fzb[:, 2, :]), (kcT, 67, fzb[:, 3, :])):
                        tr.dma("sp", dst[:, row, t0:t0 + 512], src, "fz", r=["fzb", "onesb"])
                tr.barrier()

        def phase2(l):
            with contextlib.ExitStack() as es:
                QT = [sbuf(es, f"QT{i}", [68, T], BF16) for i in range(2)]
                KT = [sbuf(es, f"KT{i}", [68, T], BF16) for i in range(2)]
                VV = [sbuf(es, f"VV{i}", [128, NT, 65], BF16) for i in range(2)]
                Pb = [sbuf(es, f"P{i}", [128, 2, 512], BF16) for i in range(3)]
                mB = sbuf(es, "mB", [128, 20, 512], BF16)
                rden = sbuf(es, "rden", [65, 1024], F32)
                bc = [sbuf(es, f"bc{i}", [64, 512], F32) for i in range(2)]
                tA = [sbuf(es, f"tA{i}", [64, 512], F32) for i in range(3)]
                ost = [sbuf(es, f"ost{i}", [64, 512], BF16) for i in range(2)]

                tr.dma("sp", mB[:], maskB_d.rearrange("k p q -> p k q"), "c0", w=["mB"])
                for i in range(2):
                    tr.op("dve", lambda e, i=i: e.memset(VV[i][:, :, 64:65], 1.0), w=[f"VV{i}"])

                units = [("A", h) for h in range(4)] + [("B", h) for h in range(6)] + [("C", h) for h in range(6)]

                def load_unit(u):
                    kind, h = units[u]
                    s = u % 2
                    if kind == "A":
                        tr.dma("sp", QT[s][0:64, :], qaT[h * 64:(h + 1) * 64, :], f"qkv{s}", w=[f"QT{s}"])
                        tr.dma("sp", KT[s][0:64, :], kaT[h * 64:(h + 1) * 64, :], f"qkv{s}", w=[f"KT{s}"])
                        vsrc = va_d
                    elif kind == "B":
                        tr.dma("sp", QT[s][0:64, :], qbT[h * 64:(h + 1) * 64, :], f"qkv{s}", w=[f"QT{s}"])
                        tr.dma("sp", KT[s][0:64, :], kbT[h * 64:(h + 1) * 64, :], f"qkv{s}", w=[f"KT{s}"])
                        vsrc = vb_d
                    else:
                        tr.dma("sp", QT[s][0:68, :], qcT[h, :, :], f"qkv{s}", w=[f"QT{s}"])
                        tr.dma("sp", KT[s][0:68, :], kcT[h, :, :], f"qkv{s}", w=[f"KT{s}"])
                        vsrc = vc_d
                    tr.dma("sp", VV[s][:, :, 0:64], vsrc[:, h * 64:(h + 1) * 64].rearrange("(n p) d -> p n d", p=128),
                           f"qkv{s}", w=[f"VV{s}"])

                SPAIR = ((0, 1), (2, 3))
                OB = (4, 5, 6)
                MISC = 7
                state = dict(n=0, no=0, npost=0)
                pending = []

                def tick():
                    for it in pending:
                        it[0] -= 1
                    while pending and pending[0][0] <= 0:
                        pending.pop(0)[1]()

                def flush():
                    while pending:
                        pending.pop(0)[1]()

                def defer(k, fn):
                    pending.append([k, fn])

                blocks = []

                def emit_pv(grp):
                    for (s, i, q0, ob, first, last, pslot, j) in grp:
                        tr.op("pe", lambda e, s=s, i=i, q0=q0, ob=ob, first=first, last=last, pslot=pslot, j=j: e.matmul(
                            ps[0:65, ob, q0:512], lhsT=VV[s][:, i, 0:65], rhs=Pb[pslot][:, j, q0:512], start=first, stop=last),
                            r=[f"VV{s}", f"P{pslot}"], w=[f"ps{ob}"])

                def pair(s, rows, J, blks, scale, ob):
                    n = state["n"]
                    state["n"] += 1
                    sp_ = SPAIR[n % 2]
                    pslot = n % 3
                    r0, r1_ = rows
                    for j, (i, q0, mask, first, last) in enumerate(blks):
                        tr.op("pe", lambda e, i=i, q0=q0, j=j: e.matmul(
                            ps[:, sp_[j], q0:512], lhsT=KT[s][r0:r1_, i * 128:(i + 1) * 128],
                            rhs=QT[s][r0:r1_, J * 512 + q0:(J + 1) * 512], start=True, stop=True),
                            r=[f"KT{s}", f"QT{s}"], w=[f"ps{sp_[j]}"])
                    qm = blks[0][1]
                    nb = len(blks)
                    tr.op("act", lambda e: e.activation(out=Pb[pslot][:, 0:nb, qm:512], in_=ps[:, sp_[0]:sp_[0] + nb, qm:512],
                                                        func=AF.Exp, scale=scale),
                          r=[f"ps{sp_[j]}" for j in range(nb)], w=[f"P{pslot}"])
                    grp = []
                    for j, (i, q0, mask, first, last) in enumerate(blks):
                        if mask is not None:
                            mk, mkey, w_ = mask
                            tr.op("dve", lambda e, j=j, q0=q0, w_=w_, mk=mk: e.tensor_tensor(
                                out=Pb[pslot][:, j, q0:q0 + w_], in0=Pb[pslot][:, j, q0:q0 + w_], in1=mk, op=ALU.mult),
                                r=[f"P{pslot}", mkey], w=[f"P{pslot}"])
                        grp.append((s, i, q0, ob, first, last, pslot, j))
                    blocks.append(grp)
                    if len(blocks) > 1:
                        emit_pv(blocks.pop(0))
                    tick()

                def run_blocks(s, rows, J, lst, scale, ob):
                    for k in range(0, len(lst), 2):
                        pair(s, rows, J, lst[k:k + 2], scale, ob)

                def drain_pv():
                    while blocks:
                        emit_pv(blocks.pop(0))

                def post_simple(ob, row0, J):
                    k = state["npost"]
                    state["npost"] += 1
                    o_ = k % 2

                    def s1():
                        tr.op("dve", lambda e: e.reciprocal(out=rden[64:65, 0:512], in_=ps[64:65, ob, :]),
                              r=[f"ps{ob}"], w=["rden"])

                    def s2():
                        tr.op("pe", lambda e: e.matmul(ps[0:64, MISC, :], lhsT=ones32[64:65, 0:64], rhs=rden[64:65, 0:512],
                                                       start=True, stop=True), r=["rden", "ones32"], w=[f"ps{MISC}"])
                        tr.op("dve", lambda e: e.tensor_copy(out=bc[0][:], in_=ps[0:64, MISC, :]), r=[f"ps{MISC}"], w=["bc0"])
                        tr.op("dve", lambda e: e.tensor_tensor(out=ost[o_][:], in0=ps[0:64, ob, :], in1=bc[0][:], op=ALU.mult),
                              r=[f"ps{ob}", "bc0"], w=[f"ost{o_}"])
                        tr.dma("sp", oT_d[row0:row0 + 64, J * 512:(J + 1) * 512], ost[o_][:], f"ost{o_}", r=[f"ost{o_}"])

                    s1()
                    defer(2, s2)

                def post_A(ob0, ob1, row0, J):
                    k = state["npost"]
                    state["npost"] += 1
                    o_ = k % 2

                    def s1():
                        tr.op("dve", lambda e: e.reciprocal(out=rden[64:65, 0:512], in_=ps[64:65, ob0, :]),
                              r=[f"ps{ob0}"], w=["rden"])
                        tr.op("dve", lambda e: e.reciprocal(out=rden[64:65, 512:1024], in_=ps[64:65, ob1, :]),
                              r=[f"ps{ob1}"], w=["rden"])

                    def s2():
                        for m, ob in ((0, ob0), (1, ob1)):
                            tr.op("pe", lambda e, m=m: e.matmul(ps[0:64, MISC, :], lhsT=ones32[64:65, 0:64],
                                                                rhs=rden[64:65, m * 512:(m + 1) * 512], start=True, stop=True),
                                  r=["rden", "ones32"], w=[f"ps{MISC}"])
                            tr.op("dve", lambda e, m=m: e.tensor_copy(out=bc[m][:], in_=ps[0:64, MISC, :]),
                                  r=[f"ps{MISC}"], w=[f"bc{m}"])
                            tr.op("dve", lambda e, m=m, ob=ob: e.tensor_tensor(out=tA[m][:], in0=ps[0:64, ob, :], in1=bc[m][:],
                                                                               op=ALU.mult),
                                  r=[f"ps{ob}", f"bc{m}"], w=[f"tA{m}"])
                        tr.op("dve", lambda e: e.scalar_tensor_tensor(out=tA[2][:], in0=tA[1][:], scalar=sm[0:64, 0:1], in1=tA[0][:],
                                                                      op0=ALU.mult, op1=ALU.add),
                              r=["tA0", "tA1", "sm"], w=["tA2"])
                        tr.op("pool", lambda e: e.tensor_tensor(out=tA[0][:], in0=tA[2][:], in1=tA[2][:], op=ALU.mult),
                              r=["tA2"], w=["tA0"])

                    def s3():
                        tr.op("pe", lambda e: e.matmul(ps[0:64, MISC, :], lhsT=ones32[0:64, 0:64], rhs=tA[0][:],
                                                       start=True, stop=True), r=["tA0", "ones32"], w=[f"ps{MISC}"])
                        tr.op("dve", lambda e: e.tensor_scalar(out=tA[1][:], in0=ps[0:64, MISC, :], scalar1=1.0 / 64, scalar2=SUBLN_EPS,
                                                               op0=ALU.mult, op1=ALU.add), r=[f"ps{MISC}"], w=["tA1"])

                    def s4():
                        tr.op("act", lambda e: e.activation(out=tA[1][:], in_=tA[1][:], func=AF.Ln), r=["tA1"], w=["tA1"])
                        tr.op("act", lambda e: e.activation(out=tA[1][:], in_=tA[1][:], func=AF.Exp, scale=-0.5), r=["tA1"], w=["tA1"])
                        tr.op("dve", lambda e: e.scalar_tensor_tensor(out=ost[o_][:], in0=tA[2][:], scalar=sm[0:64, 1:2], in1=tA[1][:],
                                                                      op0=ALU.mult, op1=ALU.mult),
                              r=["tA2", "tA1", "sm"], w=[f"ost{o_}"])
                        tr.dma("sp", oT_d[row0:row0 + 64, J * 512:(J + 1) * 512], ost[o_][:], f"ost{o_}", r=[f"ost{o_}"])

                    s1()
                    defer(2, s2)
                    defer(4, s3)
                    defer(6, s4)

                load_unit(0)
                for u, (kind, h) in enumerate(units):
                    s = u % 2
                    if u + 1 < len(units):
                        load_unit(u + 1)
                    for J in range(NQ):
                        nk = 4 * J + 4

                        def mk_list(i0, maskfn):
                            lst = []
                            for i in range(i0, nk):
                                a = i - 4 * J
                                q0 = 128 * a if a > 0 else 0
                                lst.append((i, q0, maskfn(i, a, q0), i == i0, i == nk - 1))
                            return lst

                        def nxt_ob():
                            ob = OB[state["no"] % 3]
                            state["no"] += 1
                            return ob

                        if kind == "A":
                            obs = []
                            for m in range(2):
                                ob = nxt_ob()
                                obs.append(ob)
                                lst = mk_list(0, lambda i, a, q0: (tri[:, :], "tri", 128) if a >= 0 else None)
                                run_blocks(s, (32 * m, 32 * m + 32), J, lst, 32 ** -0.5, ob)
                            drain_pv()
                            post_A(obs[0], obs[1], h * 64, J)
                        elif kind == "C":
                            ob = nxt_ob()
                            lst = mk_list(0, lambda i, a, q0: (tri[:, :], "tri", 128) if a >= 0 else None)
                            run_blocks(s, (0, 68), J, lst, 1.0, ob)
                            drain_pv()
                            post_simple(ob, 256 + 384 + h * 64, J)
                        else:
                            ob = nxt_ob()
                            lst = mk_list(max(0, 4 * J - 16),
                                          lambda i, a, q0: (mB[:, (4 * J - i) + 3, q0:512], "mB", 512 - q0))
                            run_blocks(s, (0, 64), J, lst, 0.125, ob)
                            drain_pv()
                            post_simple(ob, 256 + h * 64, J)
                flush()
                tr.barrier()

        def phase3(l, x_src):
            with contextlib.ExitStack() as es:
                wout = sbuf(es, "wout", [128, 8, D], BF16)
                wup = sbuf(es, "wup", [128, 8, 2 * DFF], BF16)
                cw = sbuf(es, "cw", [128, NFF, 4], F32)
                oTt = sbuf(es, "oTt", [128, 8, 512], BF16)
                xin = [sbuf(es, f"xin{i}", [128, D], F32) for i in range(2)]
                junk = sbuf(es, "junk", [128, D], BF16)
                ss = sbuf(es, "ss", [128, 4], F32)
                t32 = sbuf(es, "t32", [128, D], F32)
                hb = sbuf(es, "hb", [128, D], BF16)
                hT = sbuf(es, "hT", [128, 8, 512], BF16)
                gbuf = [sbuf(es, f"gbuf{i}", [128, 514], F32) for i in range(2)]
                acc = [sbuf(es, f"acc{i}", [128, 512], F32) for i in range(2)]
                sg = [sbuf(es, f"sg{i}", [128, 512], F32) for i in range(2)]
                hal = sbuf(es, "hal", [128, NFF, 2], F32)
                ast = [sbuf(es, f"ast{i}", [128, 512], BF16) for i in range(4)]

                load_w_bf16(wout, w_out[l].rearrange("(kc p) n -> p kc n", p=128), 8, "wout", "wl")
                load_w_bf16(wup, w_up[l].rearrange("(kc p) n -> p kc n", p=128), 8, "wup", "wl")
                for k in range(3):
                    tr.dma("sp", cw[:, :, k:k + 1], conv_w[l, k:k + 1, :].rearrange("o (f p) -> p f o", p=128), "c0", w=["cw"],
                           allow_slow_non_contiguous=True)
                tr.dma("sp", cw[:, :, 3:4], conv_b[l:l + 1, :].rearrange("o (f p) -> p f o", p=128), "c0", w=["cw"],
                       allow_slow_non_contiguous=True)
                tr.op("dve", lambda e: e.memset(hal[:], 0.0), w=["hal"])

                nx = 0
                for tt in range(NQ):
                    t0 = tt * 512
                    tr.dma("sp", oTt[:], oT_d[:, t0:t0 + 512].rearrange("(kc p) t -> p kc t", p=128), "oTt", w=["oTt"])
                    for s in range(4):
                        xs = nx % 2
                        nx += 1
                        xk = f"xin{xs}"
                        tr.dma("sp", xin[xs][:], x_src[t0 + s * 128:t0 + (s + 1) * 128, :], xk, w=[xk])
                        for j in range(2):
                            b = 2 + j
                            for kc in range(8):
                                tr.op("pe", lambda e, kc=kc, j=j, b=b, s=s: e.matmul(
                                    PS(b), lhsT=oTt[:, kc, s * 128:(s + 1) * 128], rhs=wout[:, kc, j * 512:(j + 1) * 512],
                                    start=(kc == 0), stop=(kc == 7)), r=["oTt", "wout"], w=[f"ps{b}"])
                            tr.op("dve", lambda e, j=j, b=b: e.tensor_tensor(
                                out=t32[:, j * 512:(j + 1) * 512], in0=PS(b), in1=mod[:, 2 * D + j * 512:2 * D + (j + 1) * 512],
                                op=ALU.mult), r=[f"ps{b}", "mod"], w=["t32"])
                        tr.op("pool", lambda e, xs=xs: e.tensor_tensor(out=xin[xs][:], in0=xin[xs][:], in1=t32[:], op=ALU.add),
                              r=["t32", xk], w=[xk])
                        tr.dma("sp", xres[t0 + s * 128:t0 + (s + 1) * 128, :], xin[xs][:], xk, r=[xk])
                        rms_to_hT((junk, ss, t32, hb), xin[xs][:], xk, s, 4 * D, 3 * D, hT, "hT", s % 2, "p3")
                    for f in range(NFF):
                        bu = 4 + (f % 2)
                        bg = 6 + (f % 2)
                        z = f % 2
                        for kc in range(8):
                            tr.op("pe", lambda e, kc=kc, f=f, bu=bu: e.matmul(
                                PS(bu), lhsT=wup[:, kc, f * 128:(f + 1) * 128], rhs=hT[:, kc, :], start=(kc == 0), stop=(kc == 7)),
                                r=["wup", "hT"], w=[f"ps{bu}"])
                        for kc in range(8):
                            tr.op("pe", lambda e, kc=kc, f=f, bg=bg: e.matmul(
                                PS(bg), lhsT=wup[:, kc, DFF + f * 128:DFF + (f + 1) * 128], rhs=hT[:, kc, :],
                                start=(kc == 0), stop=(kc == 7)), r=["wup", "hT"], w=[f"ps{bg}"])
                        gk = f"gbuf{z}"
                        tr.op("pool", lambda e, z=z, f=f: e.tensor_copy(out=gbuf[z][:, 0:2], in_=hal[:, f, :]), r=["hal"], w=[gk])
                        tr.op("act", lambda e, z=z, bg=bg: e.activation(out=gbuf[z][:, 2:514], in_=PS(bg), func=AF.Copy),
                              r=[f"ps{bg}"], w=[gk])
                        tr.op("pool", lambda e, z=z, f=f: e.tensor_copy(out=hal[:, f, :], in_=gbuf[z][:, 512:514]), r=[gk], w=["hal"])
                        ak = f"acc{z}"
                        tr.op("dve", lambda e, z=z, f=f: e.tensor_scalar(
                            out=acc[z][:], in0=gbuf[z][:, 2:514], scalar1=cw[:, f, 2:3], scalar2=cw[:, f, 3:4],
                            op0=ALU.mult, op1=ALU.add), r=[gk, "cw"], w=[ak])
                        tr.op("dve", lambda e, z=z, f=f: e.scalar_tensor_tensor(
                            out=acc[z][:], in0=gbuf[z][:, 1:513], scalar=cw[:, f, 1:2], in1=acc[z][:], op0=ALU.mult, op1=ALU.add),
                            r=[gk, "cw", ak], w=[ak])
                        tr.op("dve", lambda e, z=z, f=f: e.scalar_tensor_tensor(
                            out=acc[z][:], in0=gbuf[z][:, 0:512], scalar=cw[:, f, 0:1], in1=acc[z][:], op0=ALU.mult, op1=ALU.add),
                            r=[gk, "cw", ak], w=[ak])
                        sk = f"sg{z}"
                        tr.op("act", lambda e, z=z: e.activation(out=sg[z][:], in_=acc[z][:], func=AF.Silu), r=[ak], w=[sk])
                        a4 = f % 4
                        tr.op("dve", lambda e, z=z, a4=a4, bu=bu: e.tensor_tensor(out=ast[a4][:], in0=PS(bu), in1=sg[z][:], op=ALU.mult),
                              r=[f"ps{bu}", sk], w=[f"ast{a4}"])
                        tr.dma("sp", aT_d[f * 128:(f + 1) * 128, t0:t0 + 512], ast[a4][:], f"ast{a4}", r=[f"ast{a4}"])
                tr.barrier()

        def phase4(l, dst, do_final, xdst=None):
            with contextlib.ExitStack() as es:
                wdn = sbuf(es, "wdn", [128, NFF, D], BF16)
                aTt = [sbuf(es, f"aTt{i}", [128, NFF, 512], BF16) for i in range(2)]
                xin = [sbuf(es, f"xin{i}", [128, D], F32) for i in range(3)]
                t32 = sbuf(es, "t32", [128, D], F32)
                junk = sbuf(es, "junk", [128, D], BF16)
                ss = sbuf(es, "ss", [128, 4], F32)
                gf = sbuf(es, "gf", [128, D], F32)
                load_w_bf16(wdn, w_down[l].rearrange("(f p) n -> p f n", p=128), NFF, "wdn", "wl")
                if do_final:
                    tr.dma("sp", gf[:], g_fin[0:1, :].partition_broadcast(128), "c0", w=["gf"])
                nx = 0
                for tt in range(NQ):
                    t0 = tt * 512
                    a_ = tt % 2
                    akey = f"aTt{a_}"
                    tr.dma("sp", aTt[a_][:], aT_d[:, t0:t0 + 512].rearrange("(f p) t -> p f t", p=128), akey, w=[akey])
                    for s in range(4):
                        xs = nx % 3
                        nx += 1
                        xk = f"xin{xs}"
                        tr.dma("sp", xin[xs][:], xres[t0 + s * 128:t0 + (s + 1) * 128, :], xk, w=[xk])
                        for j in range(2):
                            b = (2 * s + j) % 4
                            for f in range(NFF):
                                tr.op("pe", lambda e, f=f, j=j, b=b, s=s, a_=a_: e.matmul(
                                    PS(b), lhsT=aTt[a_][:, f, s * 128:(s + 1) * 128], rhs=wdn[:, f, j * 512:(j + 1) * 512],
                                    start=(f == 0), stop=(f == NFF - 1)), r=[akey, "wdn"], w=[f"ps{b}"])
                            tr.op("dve", lambda e, j=j, b=b: e.tensor_tensor(
                                out=t32[:, j * 512:(j + 1) * 512], in0=PS(b), in1=mod[:, 5 * D + j * 512:5 * D + (j + 1) * 512],
                                op=ALU.mult), r=[f"ps{b}", "mod"], w=["t32"])
                        tr.op("pool", lambda e, xs=xs: e.tensor_tensor(out=xin[xs][:], in0=xin[xs][:], in1=t32[:], op=ALU.add),
                              r=["t32", xk], w=[xk])
                        if do_final:
                            tr.op("act", lambda e, xs=xs: e.activation(out=junk[:], in_=xin[xs][:], func=AF.Square, accum_out=ss[:, 0:1]),
                                  r=[xk], w=["junk", "ss"])
                            tr.op("dve", lambda e: e.tensor_scalar(out=ss[:, 1:2], in0=ss[:, 0:1], scalar1=1.0 / D, scalar2=NORM_EPS,
                                                                   op0=ALU.mult, op1=ALU.add), r=["ss"], w=["ss"])
                            tr.op("act", lambda e: e.activation(out=ss[:, 2:3], in_=ss[:, 1:2], func=AF.Sqrt), r=["ss"], w=["ss"])
                            tr.op("dve", lambda e: e.reciprocal(out=ss[:, 3:4], in_=ss[:, 2:3]), r=["ss"], w=["ss"])
                            if xdst is not None:
                                tr.dma("sp", xdst[t0 + s * 128:t0 + (s + 1) * 128, :], xin[xs][:], xk, r=[xk])
                            tr.op("dve", lambda e, xs=xs: e.scalar_tensor_tensor(
                                out=t32[:], in0=xin[xs][:], scalar=ss[:, 3:4], in1=gf[:], op0=ALU.mult, op1=ALU.mult),
                                r=[xk, "ss", "gf"], w=["t32"])
                            tr.dma("sp", dst[t0 + s * 128:t0 + (s + 1) * 128, :], t32[:], "t32st", r=["t32"])
                        else:
                            tr.dma("sp", dst[t0 + s * 128:t0 + (s + 1) * 128, :], xin[xs][:], xk, r=[xk])
                tr.barrier()

        for l in range(L):
            phase0(l)
            phase1(l, x_d if l == 0 else xres)
            phase2(l)
            phase3(l, x_d if l == 0 else xres)
            last = (l == L - 1)
            phase4(l, y_d if last else xres, last and final, xo_d if last else None)
        tr.barrier()
        print("kernel instructions (approx):", tr.ninst, flush=True)
    return nc


def _rope_tab(T, dim, theta=10000.0):
    inv = (1.0 / (np.float32(theta) ** (np.arange(0, dim, 2, dtype=np.float32) / np.float32(dim)))).astype(np.float32)
    ang = (np.arange(T, dtype=np.float32)[:, None] * inv[None, :]).astype(np.float32)
    cos = np.cos(ang).astype(np.float32).T
    sin = np.sin(ang).astype(np.float32).T
    tab = np.empty((2, dim, T), np.float32)
    tab[0] = np.concatenate([cos, cos], 0)
    tab[1] = np.concatenate([-sin, sin], 0)
    return tab


def _consts(T):
    kk = np.arange(128)[:, None]
    qq = np.arange(512)[None, :]
    mB = np.zeros((20, 128, 512), np.float32)
    for di in range(20):
        dist = (di - 3) * 128 + qq - kk
        m = ((dist >= 0) & (dist <= 128)).astype(np.float32)
        m += ((dist >= 0) & (dist <= 512) & (dist % 4 == 0)).astype(np.float32)
        m += ((dist >= 0) & (dist <= 2048) & (dist % 16 == 0)).astype(np.float32)
        mB[di] = m
    tri = (np.arange(128)[:, None] <= np.arange(128)[None, :]).astype(np.float32)
    return dict(
        ropeA=_rope_tab(T, 32), ropeB=_rope_tab(T, 64),
        maskB=mB.astype(ml_dtypes.bfloat16), tri=tri.astype(ml_dtypes.bfloat16),
        identb=np.eye(128, dtype=np.float32).astype(ml_dtypes.bfloat16),
    )


def _rot_perm():
    idx = []
    for base in (QA0, KA0):
        for u in range(8):
            for d in range(32):
                idx.append(base + u * 32 + (d + 16) % 32)
    for base in (QB0, KB0):
        for h in range(6):
            for d in range(64):
                idx.append(base + h * 64 + (d + 32) % 64)
    return np.asarray(idx, np.int64)


def make_in_maps(inputs, T, layers, ncores):
    f = lambda a: np.ascontiguousarray(np.asarray(a, dtype=np.float32))
    L = len(layers)
    ls = list(layers)
    perm = _rot_perm()
    w_in = f(inputs["w_in"])[ls]
    shared = dict(
        w_mod=f(inputs["w_mod"])[ls], b_mod=f(inputs["b_mod"])[ls], g_attn=f(inputs["g_attn"])[ls],
        w_in=np.ascontiguousarray(w_in), w_inr=np.ascontiguousarray(w_in[:, :, perm]),
        diff_lambda=f(inputs["diff_lambda"])[ls].reshape(L, 128), subln_g=f(inputs["subln_g"])[ls],
        forget_bias=f(inputs["forget_bias"])[ls], w_out=f(inputs["w_out"])[ls], g_mlp=f(inputs["g_mlp"])[ls],
        w_up=f(inputs["w_up"])[ls], conv_w=f(inputs["conv_w"])[ls], conv_b=f(inputs["conv_b"])[ls],
        w_down=f(inputs["w_down"])[ls], g_final=f(inputs["g_final"]).reshape(1, D),
        lamc=np.asarray([[0.8 - 0.6 * math.exp(-0.3 * l), 1.0 - (0.8 - 0.6 * math.exp(-0.3 * l))] for l in ls], np.float32),
    )
    shared.update(_consts(T))
    x = np.asarray(inputs["x"], dtype=np.float32)
    c = np.asarray(inputs["c"], dtype=np.float32)
    maps = []
    for b in range(ncores):
        m = dict(shared)
        m["x"] = np.ascontiguousarray(x[b, :T])
        m["c"] = np.ascontiguousarray(c[b])
        maps.append(m)
    return maps


FUSED = False


def kernel(x, c, w_mod, b_mod, g_attn, w_in, diff_lambda, subln_g, forget_bias, w_out,
           g_mlp, w_up, conv_w, conv_b, w_down, g_final):
    inputs = dict(x=x, c=c, w_mod=w_mod, b_mod=b_mod, g_attn=g_attn, w_in=w_in, diff_lambda=diff_lambda,
                  subln_g=subln_g, forget_bias=forget_bias, w_out=w_out, g_mlp=g_mlp, w_up=w_up,
                  conv_w=conv_w, conv_b=conv_b, w_down=w_down, g_final=g_final)
    B, T, _ = np.asarray(x).shape
    L = np.asarray(w_in).shape[0]
    assert B == NCORES
    if FUSED:
        nc = build(T, L, final=True)
        maps = make_in_maps(inputs, T, range(L), NCORES)
        res = run_bass_kernel_spmd(nc, maps, core_ids=list(range(NCORES)))
        return np.stack([np.asarray(r["y"], dtype=np.float32) for r in res.results], axis=0)
    nc = build(T, 1, final=True, emit_x=True)
    cur = np.asarray(x, dtype=np.float32)
    out = None
    for l in range(L):
        inputs["x"] = cur
        maps = make_in_maps(inputs, T, [l], NCORES)
        res = run_bass_kernel_spmd(nc, maps, core_ids=list(range(NCORES)))
        if l == L - 1:
            out = np.stack([np.asarray(r["y"], dtype=np.float32) for r in res.results], axis=0)
        else:
            cur = np.stack([np.asarray(r["xo"], dtype=np.float32) for r in res.results], axis=0)
    return out
```

```python
import contextlib
import math

import ml_dtypes
import numpy as np

import concourse.bass as bass
import concourse.mybir as mybir
from concourse.bass_utils import run_bass_kernel_spmd

F32 = mybir.dt.float32
BF16 = mybir.dt.bfloat16
AF = mybir.ActivationFunctionType
ALU = mybir.AluOpType
AX = mybir.AxisListType

D = 1024
DFF = 2816
NFF = 22
INC = 3078
ROTC = 1280
NORM_EPS = 1e-6
SUBLN_EPS = 1e-5
NCORES = 8
SAME_ENGINE_SYNC = True

QA0, KA0, VA0, QB0, KB0, VB0, QC0, KC0, VC0, FZ0 = 0, 256, 512, 768, 1152, 1536, 1920, 2304, 2688, 3072


class Tracker:
    def __init__(self, nc, es):
        self.nc = nc
        self.E = dict(pe=nc.tensor, act=nc.scalar, dve=nc.vector, pool=nc.gpsimd, sp=nc.sync)
        self.sem = {}
        self.cnt = {}
        self.es = es
        for e in ("pe", "act", "dve", "pool"):
            self.sem[e] = es.enter_context(nc.semaphore("prog_" + e))
            self.cnt[e] = 0
        self.waited = {e: {} for e in self.E}
        self.clock = {}
        self.bw = {}
        self.br = {}
        self.ninst = 0

    def dsem(self, name):
        if name not in self.sem:
            self.sem[name] = self.es.enter_context(self.nc.semaphore("d_" + name))
            self.cnt[name] = 0
        return name

    def _deps(self, r, w):
        d = {}
        raw = {}

        def add(k, v, is_raw=False):
            if k not in self.E:
                v = self.cnt[k]
            if d.get(k, 0) < v:
                d[k] = v
            if is_raw and raw.get(k, 0) < v:
                raw[k] = v

        for b in r:
            t = self.bw.get(b)
            if t:
                add(t[0], t[1], True)
        for b in w:
            t = self.bw.get(b)
            if t:
                add(*t)
            for k, v in self.br.get(b, {}).items():
                add(k, v)
        self._raw = raw
        return d

    def _waits(self, e, d):
        order = sorted(d.items(), key=lambda kv: (0 if kv[0] in ("pe", "dve", "act", "pool") else 1))
        for k, v in order:
            if k == e:
                if e == "pe" or not SAME_ENGINE_SYNC:
                    continue
                v = self._raw.get(k, 0)
                if v == 0:
                    continue
            if self.waited[e].get(k, 0) >= v:
                continue
            self.E[e].wait_ge(self.sem[k], v)
            self.ninst += 1
            self._learn(e, k, v)

    def _learn(self, e, k, v):
        we = self.waited[e]
        we[k] = v
        for kk, vv in self.clock.get((k, v), {}).items():
            if kk != e and we.get(kk, 0) < vv:
                we[kk] = vv

    def _record(self, tok, r, w):
        k, v = tok
        for b in r:
            self.br.setdefault(b, {})[k] = v
        for b in w:
            self.bw[b] = tok
            self.br[b] = {}

    def op(self, e, fn, r=(), w=()):
        self._waits(e, self._deps(r, w))
        inst = fn(self.E[e])
        self.cnt[e] += 1
        inst.then_inc(self.sem[e], 1)
        self.ninst += 1
        self.clock[(e, self.cnt[e])] = dict(self.waited[e])
        self._record((e, self.cnt[e]), r, w)

    def dma(self, q, out, in_, sem, r=(), w=(), **kw):
        self.dsem(sem)
        self._waits(q, self._deps(r, w))
        inst = self.E[q].dma_start(out=out, in_=in_, **kw)
        self.cnt[sem] += 16
        inst.then_inc(self.sem[sem], 16)
        self.ninst += 1
        self.clock[(sem, self.cnt[sem])] = dict(self.waited[q])
        self._record((sem, self.cnt[sem]), r, w)

    def barrier(self, engines=("pe", "act", "dve", "pool", "sp")):
        for e in engines:
            for k, v in self.cnt.items():
                if v == 0 or k == e:
                    continue
                if self.waited[e].get(k, 0) >= v:
                    continue
                self.E[e].wait_ge(self.sem[k], v)
                self.waited[e][k] = v
        self.clock = {}
        self.bw = {}
        self.br = {}


def build(T, L, final=True, dbg=False, emit_x=False):
    NQ = T // 512
    NT = T // 128
    nc = bass.Bass("TRN2", target_bir_lowering=False)

    def din(name, shape, dt=F32):
        return nc.dram_tensor(name, list(shape), dt, kind="ExternalInput").ap()

    def dscr(name, shape, dt):
        kind = "ExternalOutput" if dbg else "Internal"
        return nc.dram_tensor(name, list(shape), dt, kind=kind).ap()

    x_d = din("x", [T, D])
    c_d = din("c", [D])
    w_mod = din("w_mod", [L, D, 6 * D])
    b_mod = din("b_mod", [L, 6 * D])
    g_attn = din("g_attn", [L, D])
    w_in = din("w_in", [L, D, INC])
    w_inr = din("w_inr", [L, D, ROTC])
    dlam = din("diff_lambda", [L, 128])
    subg = din("subln_g", [L, 64])
    fbias = din("forget_bias", [L, 6])
    w_out = din("w_out", [L, D, D])
    g_mlp = din("g_mlp", [L, D])
    w_up = din("w_up", [L, D, 2 * DFF])
    conv_w = din("conv_w", [L, 3, DFF])
    conv_b = din("conv_b", [L, DFF])
    w_down = din("w_down", [L, DFF, D])
    g_fin = din("g_final", [1, D])
    lamc = din("lamc", [L, 2])
    ropeA = din("ropeA", [2, 32, T])
    ropeB = din("ropeB", [2, 64, T])
    maskB_d = din("maskB", [20, 128, 512], BF16)
    tri_d = din("tri", [128, 128], BF16)
    ident_d = din("identb", [128, 128], BF16)

    y_d = nc.dram_tensor("y", [T, D], F32, kind="ExternalOutput").ap()
    xo_d = nc.dram_tensor("xo", [T, D], F32, kind="ExternalOutput").ap() if emit_x else None

    xres = dscr("xres", [T, D], F32)
    qaT = dscr("qaT", [256, T], BF16)
    kaT = dscr("kaT", [256, T], BF16)
    qbT = dscr("qbT", [384, T], BF16)
    kbT = dscr("kbT", [384, T], BF16)
    qcT = dscr("qcT", [6, 68, T], BF16)
    kcT = dscr("kcT", [6, 68, T], BF16)
    va_d = dscr("va", [T, 256], BF16)
    vb_d = dscr("vb", [T, 384], BF16)
    vc_d = dscr("vc", [T, 384], BF16)
    oT_d = dscr("oT", [D, T], BF16)
    aT_d = dscr("aT", [DFF, T], BF16)

    top = contextlib.ExitStack()
    with top:
        tr = Tracker(nc, top)

        uniq = [0]

        def sbuf(es, name, shape, dt):
            uniq[0] += 1
            return es.enter_context(nc.sbuf_tensor(f"{name}_u{uniq[0]}", list(shape), dt))

        ps = top.enter_context(nc.psum_tensor("ps", [128, 8, 512], F32))

        def PS(b):
            return ps[:, b, :]

        def PSB(b):
            return ps[:, b, :].bitcast(BF16)

        mod = sbuf(top, "mod", [128, 6 * D], F32)
        SC = sbuf(top, "SC", [128, 8, 128], F32)
        identb = sbuf(top, "identb_s", [128, 128], BF16)
        tri = sbuf(top, "tri_s", [128, 128], BF16)
        ones32 = sbuf(top, "ones32", [128, 128], F32)
        onesb = sbuf(top, "onesb", [128, 512], BF16)
        sm = sbuf(top, "sm", [128, 16], F32)
        lamt = sbuf(top, "lamt", [128, 2], F32)

        tr.dma("sp", identb[:], ident_d[:, :], "c0", w=["identb"])
        tr.dma("sp", tri[:], tri_d[:, :], "c0", w=["tri"])
        tr.op("dve", lambda e: e.memset(ones32[:], 1.0), w=["ones32"])
        tr.op("dve", lambda e: e.memset(onesb[:], 1.0), w=["onesb"])

        with contextlib.ExitStack() as es:
            ct = sbuf(es, "ct", [128, 8], F32)
            sct = sbuf(es, "sct", [128, 8], F32)
            tr.dma("sp", ct[:], c_d.rearrange("(kc p) -> p kc", p=128), "c0", w=["ct"],
                   allow_slow_non_contiguous=True)
            tr.op("act", lambda e: e.activation(out=sct[:], in_=ct[:], func=AF.Silu), r=["ct"], w=["sct"])
            for kc in range(8):
                tr.op("dve", lambda e, kc=kc: e.tensor_scalar(
                    out=SC[:, kc, :], in0=ones32[:, :], scalar1=sct[:, kc:kc + 1], scalar2=None, op0=ALU.mult),
                    r=["sct", "ones32"], w=["SC"])
            tr.barrier()

        def phase0(l):
            with contextlib.ExitStack() as es:
                wm = [sbuf(es, f"wm{i}", [128, 8, 512], F32) for i in range(2)]
                bm = [sbuf(es, f"bm{i}", [128, 512], F32) for i in range(2)]
                gb = sbuf(es, "gb", [128, D], F32)
                dl = sbuf(es, "dl", [64, 128], F32)
                tmp = sbuf(es, "tmp0", [64, 64], F32)
                wv = w_mod[l].rearrange("(kc p) n -> p kc n", p=128)
                for j in range(12):
                    s = j % 2
                    tr.dma("sp", wm[s][:], wv[:, :, j * 512:(j + 1) * 512], f"wm{s}", w=[f"wm{s}"])
                    tr.dma("pool", bm[s][:], b_mod[l:l + 1, j * 512:(j + 1) * 512].partition_broadcast(128),
                           f"bm{s}", w=[f"bm{s}"])
                    b = j % 2
                    for kc in range(8):
                        tr.op("pe", lambda e, kc=kc, s=s, b=b: e.matmul(
                            PS(b), lhsT=SC[:, kc, :], rhs=wm[s][:, kc, :], start=(kc == 0), stop=(kc == 7)),
                            r=[f"wm{s}", "SC"], w=[f"ps{b}"])
                    tr.op("dve", lambda e, s=s, b=b, j=j: e.tensor_tensor(
                        out=mod[:, j * 512:(j + 1) * 512], in0=PS(b), in1=bm[s][:], op=ALU.add),
                        r=[f"ps{b}", f"bm{s}"], w=["mod"])
                for (gsrc, off) in ((g_attn, D), (g_mlp, 4 * D)):
                    tr.dma("sp", gb[:], gsrc[l:l + 1, :].partition_broadcast(128), "c0", w=["gb"])
                    tr.op("dve", lambda e, off=off: e.scalar_tensor_tensor(
                        out=mod[:, off:off + D], in0=mod[:, off:off + D], scalar=1.0, in1=gb[:],
                        op0=ALU.add, op1=ALU.mult), r=["gb", "mod"], w=["mod"])
                tr.dma("sp", dl[:], dlam[l:l + 1, :].partition_broadcast(64), "c0", w=["dl"])
                tr.dma("sp", lamt[0:64, :], lamc[l:l + 1, :].partition_broadcast(64), "c0", w=["lamt"])
                tr.dma("sp", sm[0:64, 1:2], subg[l:l + 1, :].rearrange("o d -> d o"), "c0", w=["sm"])
                tr.dma("sp", sm[0:6, 2:3], fbias[l:l + 1, :].rearrange("o d -> d o"), "c0", w=["sm"])
                tr.op("dve", lambda e: e.tensor_tensor(out=tmp[:, 0:32], in0=dl[:, 0:32], in1=dl[:, 32:64], op=ALU.mult),
                      r=["dl"], w=["tmp0"])
                tr.op("dve", lambda e: e.tensor_tensor(out=tmp[:, 32:64], in0=dl[:, 64:96], in1=dl[:, 96:128], op=ALU.mult),
                      r=["dl", "tmp0"], w=["tmp0"])
                tr.op("dve", lambda e: e.tensor_reduce(out=sm[0:64, 3:4], in_=tmp[:, 0:32], axis=AX.X, op=ALU.add),
                      r=["tmp0", "sm"], w=["sm"])
                tr.op("dve", lambda e: e.tensor_reduce(out=sm[0:64, 4:5], in_=tmp[:, 32:64], axis=AX.X, op=ALU.add),
                      r=["tmp0", "sm"], w=["sm"])
                tr.op("act", lambda e: e.activation(out=sm[0:64, 5:7], in_=sm[0:64, 3:5], func=AF.Exp), r=["sm"], w=["sm"])
                tr.op("dve", lambda e: e.tensor_tensor(out=sm[0:64, 7:8], in0=sm[0:64, 6:7], in1=sm[0:64, 5:6], op=ALU.subtract),
                      r=["sm"], w=["sm"])
                tr.op("dve", lambda e: e.tensor_tensor(out=sm[0:64, 0:1], in0=sm[0:64, 7:8], in1=lamt[0:64, 0:1], op=ALU.subtract),
                      r=["sm", "lamt"], w=["sm"])
                tr.op("dve", lambda e: e.tensor_tensor(out=sm[0:64, 1:2], in0=sm[0:64, 1:2], in1=lamt[0:64, 1:2], op=ALU.mult),
                      r=["sm", "lamt"], w=["sm"])
                tr.op("dve", lambda e: e.tensor_scalar(out=sm[0:6, 2:3], in0=sm[0:6, 2:3], scalar1=-1.0, scalar2=None, op0=ALU.mult),
                      r=["sm"], w=["sm"])
                tr.barrier()

        def rms_to_hT(es_bufs, xin_t, xkey, s, gmod_off, shift_off, hT, hTkey, psb, tag):
            junk, ss, t32, hb = es_bufs
            tr.op("act", lambda e: e.activation(out=junk[:], in_=xin_t, func=AF.Square, accum_out=ss[:, 0:1]),
                  r=[xkey], w=["junk", "ss"])
            tr.op("dve", lambda e: e.tensor_scalar(out=ss[:, 1:2], in0=ss[:, 0:1], scalar1=1.0 / D, scalar2=NORM_EPS,
                                                   op0=ALU.mult, op1=ALU.add), r=["ss"], w=["ss"])
            tr.op("act", lambda e: e.activation(out=ss[:, 2:3], in_=ss[:, 1:2], func=AF.Sqrt), r=["ss"], w=["ss"])
            tr.op("dve", lambda e: e.reciprocal(out=ss[:, 3:4], in_=ss[:, 2:3]), r=["ss"], w=["ss"])
            tr.op("dve", lambda e: e.scalar_tensor_tensor(out=t32[:], in0=xin_t, scalar=ss[:, 3:4],
                                                          in1=mod[:, gmod_off:gmod_off + D], op0=ALU.mult, op1=ALU.mult),
                  r=[xkey, "ss", "mod"], w=["t32"])
            tr.op("pool", lambda e: e.tensor_tensor(out=hb[:], in0=t32[:], in1=mod[:, shift_off:shift_off + D], op=ALU.add),
                  r=["t32", "mod"], w=["hb"])
            for kc in range(8):
                tr.op("pe", lambda e, kc=kc: e.transpose(out=PSB(psb)[:, kc * 128:(kc + 1) * 128],
                                                         in_=hb[:, kc * 128:(kc + 1) * 128], identity=identb[:]),
                      r=["hb", "identb"], w=[f"ps{psb}"])
            tr.op("act", lambda e: e.activation(out=hT[:, :, s * 128:(s + 1) * 128],
                                                in_=PSB(psb).rearrange("p (kc t) -> p kc t", kc=8), func=AF.Copy),
                  r=[f"ps{psb}"], w=[hTkey])

        def load_w_bf16(dst, src_view, nk, key, sem):
            for kc in range(nk):
                tr.dma("pool", dst[:, kc, :], src_view[:, kc, :], sem, w=[key])

        def phase1(l, x_src):
            with contextlib.ExitStack() as es:
                win = sbuf(es, "win", [128, 8, INC], BF16)
                winr = sbuf(es, "winr", [128, 8, ROTC], BF16)
                xin = [sbuf(es, f"xin{i}", [128, D], F32) for i in range(3)]
                junk = sbuf(es, "junk", [128, D], BF16)
                ss = sbuf(es, "ss", [128, 4], F32)
                t32 = sbuf(es, "t32", [128, D], F32)
                hb = sbuf(es, "hb", [128, D], BF16)
                hT = [sbuf(es, f"hT{i}", [128, 8, 512], BF16) for i in range(2)]
                tabA = sbuf(es, "tabA", [128, 2, 512], F32)
                tabB = sbuf(es, "tabB", [128, 2, 512], F32)
                r1 = [sbuf(es, f"r1_{i}", [128, 512], F32) for i in range(2)]
                r2 = [sbuf(es, f"r2_{i}", [128, 512], F32) for i in range(2)]
                qst = [sbuf(es, f"qst{i}", [128, 512], BF16) for i in range(4)]
                vst = sbuf(es, "vst", [128, 4, 1024], BF16)
                fzt = sbuf(es, "fzt", [6, 8, 512], F32)
                fzb = sbuf(es, "fzb", [6, 4, 512], BF16)
                carry = sbuf(es, "carry", [6, 1], F32)

                load_w_bf16(win, w_in[l].rearrange("(kc p) n -> p kc n", p=128), 8, "win", "wl")
                load_w_bf16(winr, w_inr[l].rearrange("(kc p) n -> p kc n", p=128), 8, "winr", "wl")
                tr.op("dve", lambda e: e.memset(carry[:], 0.0), w=["carry"])
                tr.op("dve", lambda e: e.memset(fzt[:, 3, :], 1.0), w=["fzt3"])

                nx = 0
                nq = 0
                chunks = []
                for j in range(2):
                    chunks.append((QA0 + j * 128, j * 128, "A", qaT, j))
                for j in range(2):
                    chunks.append((KA0 + j * 128, 256 + j * 128, "A", kaT, j))
                for j in range(3):
                    chunks.append((QB0 + j * 128, 512 + j * 128, "B", qbT, j))
                for j in range(3):
                    chunks.append((KB0 + j * 128, 896 + j * 128, "B", kbT, j))
                for j in range(3):
                    chunks.append((QC0 + j * 128, None, "Cq", qcT, j))
                for j in range(3):
                    chunks.append((KC0 + j * 128, None, "Ck", kcT, j))

                for tt in range(NQ):
                    t0 = tt * 512
                    hs = tt % 2
                    hTt = hT[hs]
                    hkey = f"hT{hs}"
                    for s in range(4):
                        xs = nx % 3
                        nx += 1
                        tr.dma("sp", xin[xs][:], x_src[t0 + s * 128:t0 + (s + 1) * 128, :], f"xin{xs}", w=[f"xin{xs}"])
                        rms_to_hT((junk, ss, t32, hb), xin[xs][:], f"xin{xs}", s, D, 0, hTt, hkey, s % 2, "p1")
                    for u in range(4):
                        tr.dma("sp", tabA[u * 32:(u + 1) * 32, :, :],
                               ropeA[:, :, t0:t0 + 512].rearrange("c d t -> d c t"), "tab", w=["tabA"])
                    for u in range(2):
                        tr.dma("sp", tabB[u * 64:(u + 1) * 64, :, :],
                               ropeB[:, :, t0:t0 + 512].rearrange("c d t -> d c t"), "tab", w=["tabB"])
                    for ci, (c0, rc0, kind, dst, j) in enumerate(chunks):
                        b = 2 + (ci % 2)
                        for kc in range(8):
                            tr.op("pe", lambda e, kc=kc, c0=c0, b=b: e.matmul(
                                PS(b), lhsT=win[:, kc, c0:c0 + 128], rhs=hTt[:, kc, :], start=(kc == 0), stop=(kc == 7)),
                                r=["win", hkey], w=[f"ps{b}"])
                        qs = nq % 4
                        nq += 1
                        if rc0 is not None:
                            br_ = 4 + (ci % 2)
                            for kc in range(8):
                                tr.op("pe", lambda e, kc=kc, rc0=rc0, br_=br_: e.matmul(
                                    PS(br_), lhsT=winr[:, kc, rc0:rc0 + 128], rhs=hTt[:, kc, :], start=(kc == 0), stop=(kc == 7)),
                                    r=["winr", hkey], w=[f"ps{br_}"])
                            tab = tabA if kind == "A" else tabB
                            tkey = "tabA" if kind == "A" else "tabB"
                            rs = ci % 2
                            tr.op("dve", lambda e, b=b, rs=rs, tab=tab: e.tensor_tensor(
                                out=r1[rs][:], in0=PS(b), in1=tab[:, 0, :], op=ALU.mult),
                                r=[f"ps{b}", tkey], w=[f"r1_{rs}"])
                            tr.op("dve", lambda e, br_=br_, rs=rs, tab=tab: e.tensor_tensor(
                                out=r2[rs][:], in0=PS(br_), in1=tab[:, 1, :], op=ALU.mult),
                                r=[f"ps{br_}", tkey], w=[f"r2_{rs}"])
                            tr.op("pool", lambda e, rs=rs, qs=qs: e.tensor_tensor(
                                out=qst[qs][:], in0=r1[rs][:], in1=r2[rs][:], op=ALU.add),
                                r=[f"r1_{rs}", f"r2_{rs}"], w=[f"qst{qs}"])
                            tr.dma("sp", dst[j * 128:(j + 1) * 128, t0:t0 + 512], qst[qs][:], f"qst{qs}", r=[f"qst{qs}"])
                        else:
                            sc = 0.125 if kind == "Cq" else 1.0
                            tr.op("act", lambda e, b=b, qs=qs, sc=sc: e.activation(
                                out=qst[qs][:], in_=PS(b), func=AF.Copy, scale=sc),
                                r=[f"ps{b}"], w=[f"qst{qs}"])
                            for hh in range(2):
                                tr.dma("sp", dst[2 * j + hh, 0:64, t0:t0 + 512], qst[qs][hh * 64:(hh + 1) * 64, :],
                                       f"qst{qs}", r=[f"qst{qs}"])
                    for s in range(4):
                        for gi, (c0, n, o0) in enumerate(((VA0, 256, 0), (VB0, 384, 256), (VC0, 384, 640))):
                            b = 6 + ((s * 3 + gi) % 2)
                            for kc in range(8):
                                tr.op("pe", lambda e, kc=kc, c0=c0, n=n, b=b, s=s: e.matmul(
                                    PS(b)[:, 0:n], lhsT=hTt[:, kc, s * 128:(s + 1) * 128], rhs=win[:, kc, c0:c0 + n],
                                    start=(kc == 0), stop=(kc == 7)), r=["win", hkey], w=[f"ps{b}"])
                            tr.op("act", lambda e, b=b, n=n, o0=o0, s=s: e.activation(
                                out=vst[:, s, o0:o0 + n], in_=PS(b)[:, 0:n], func=AF.Copy), r=[f"ps{b}"], w=["vst"])
                    for (dst, n, o0) in ((va_d, 256, 0), (vb_d, 384, 256), (vc_d, 384, 640)):
                        tr.dma("sp", dst[t0:t0 + 512, :].rearrange("(s p) c -> p s c", p=128), vst[:, :, o0:o0 + n],
                               "vst", r=["vst"])
                    b = 6
                    for kc in range(8):
                        tr.op("pe", lambda e, kc=kc: e.matmul(
                            PS(6)[0:6, :], lhsT=win[:, kc, FZ0:FZ0 + 6], rhs=hTt[:, kc, :], start=(kc == 0), stop=(kc == 7)),
                            r=["win", hkey], w=["ps6"])
                    tr.op("act", lambda e: e.activation(out=fzt[:, 0, :], in_=PS(6)[0:6, :], func=AF.Exp,
                                                        scale=-1.0, bias=sm[0:6, 2:3]), r=["ps6", "sm"], w=["fzt0"])
                    tr.op("act", lambda e: e.activation(out=fzt[:, 1, :], in_=fzt[:, 0, :], func=AF.Ln, bias=1.0),
                          r=["fzt0"], w=["fzt1"])
                    tr.op("dve", lambda e: e.tensor_tensor_scan(out=fzt[:, 2, :], data0=fzt[:, 3, :], data1=fzt[:, 1, :],
                                                                initial=carry[:, 0:1], op0=ALU.mult, op1=ALU.subtract),
                          r=["fzt1", "fzt3", "carry", "fzb"], w=["fzt2"])
                    tr.op("dve", lambda e: e.tensor_copy(out=carry[:], in_=fzt[:, 2, 511:512]), r=["fzt2"], w=["carry"])
                    tr.op("dve", lambda e: e.tensor_copy(out=fzb[:, 0, :], in_=fzt[:, 2, :]), r=["fzt2"], w=["fzb"])
                    tr.op("dve", lambda e: e.tensor_tensor(out=fzb[:, 1, :], in0=fzt[:, 2, :], in1=fzb[:, 0, :], op=ALU.subtract),
                          r=["fzt2", "fzb"], w=["fzb"])
                    tr.op("dve", lambda e: e.tensor_scalar(out=fzb[:, 2:4, :], in0=fzb[:, 0:2, :], scalar1=-1.0, scalar2=None,
                                                           op0=ALU.mult), r=["fzb"], w=["fzb"])
                    for (dst, row, src) in ((qcT, 64, fzb[:, 0, :]), (qcT, 65, fzb[:, 1, :]), (qcT, 66, onesb[0:6, :]),
                                            (qcT, 67, onesb[0:6, :]), (kcT, 64, onesb[0:6, :]), (kcT, 65, onesb[0:6, :]),
                                            (kcT, 66, fzb[:, 2, :]), (kcT, 67, fzb[:, 3, :])):
                        tr.dma("sp", dst[:, row, t0:t0 + 512], src, "fz", r=["fzb", "onesb"])
                tr.barrier()

        def phase2(l):
            with contextlib.ExitStack() as es:
                QT = [sbuf(es, f"QT{i}", [68, T], BF16) for i in range(2)]
                KT = [sbuf(es, f"KT{i}", [68, T], BF16) for i in range(2)]
                VV = [sbuf(es, f"VV{i}", [128, NT, 65], BF16) for i in range(2)]
                Pb = [sbuf(es, f"P{i}", [128, 2, 512], BF16) for i in range(3)]
                mB = sbuf(es, "mB", [128, 20, 512], BF16)
                rden = sbuf(es, "rden", [65, 1024], F32)
                bc = [sbuf(es, f"bc{i}", [64, 512], F32) for i in range(2)]
                tA = [sbuf(es, f"tA{i}", [64, 512], F32) for i in range(3)]
                ost = [sbuf(es, f"ost{i}", [64, 512], BF16) for i in range(2)]

                tr.dma("sp", mB[:], maskB_d.rearrange("k p q -> p k q"), "c0", w=["mB"])
                for i in range(2):
                    tr.op("dve", lambda e, i=i: e.memset(VV[i][:, :, 64:65], 1.0), w=[f"VV{i}"])

                units = [("A", h) for h in range(4)] + [("B", h) for h in range(6)] + [("C", h) for h in range(6)]

                def load_unit(u):
                    kind, h = units[u]
                    s = u % 2
                    if kind == "A":
                        tr.dma("sp", QT[s][0:64, :], qaT[h * 64:(h + 1) * 64, :], f"qkv{s}", w=[f"QT{s}"])
                        tr.dma("sp", KT[s][0:64, :], kaT[h * 64:(h + 1) * 64, :], f"qkv{s}", w=[f"KT{s}"])
                        vsrc = va_d
                    elif kind == "B":
                        tr.dma("sp", QT[s][0:64, :], qbT[h * 64:(h + 1) * 64, :], f"qkv{s}", w=[f"QT{s}"])
                        tr.dma("sp", KT[s][0:64, :], kbT[h * 64:(h + 1) * 64, :], f"qkv{s}", w=[f"KT{s}"])
                        vsrc = vb_d
                    else:
                        tr.dma("sp", QT[s][0:68, :], qcT[h, :, :], f"qkv{s}", w=[f"QT{s}"])
                        tr.dma("sp", KT[s][0:68, :], kcT[h, :, :], f"qkv{s}", w=[f"KT{s}"])
                        vsrc = vc_d
                    tr.dma("sp", VV[s][:, :, 0:64], vsrc[:, h * 64:(h + 1) * 64].rearrange("(n p) d -> p n d", p=128),
                           f"qkv{s}", w=[f"VV{s}"])

                SPAIR = ((0, 1), (2, 3))
                OB = (4, 5, 6)
                MISC = 7
                state = dict(n=0, no=0, npost=0)
                pending = []

                def tick():
                    for it in pending:
                        it[0] -= 1
                    while pending and pending[0][0] <= 0:
                        pending.pop(0)[1]()

                def flush():
                    while pending:
                        pending.pop(0)[1]()

                def defer(k, fn):
                    pending.append([k, fn])

                blocks = []

                def emit_pv(grp):
                    for (s, i, q0, ob, first, last, pslot, j) in grp:
                        tr.op("pe", lambda e, s=s, i=i, q0=q0, ob=ob, first=first, last=last, pslot=pslot, j=j: e.matmul(
                            ps[0:65, ob, q0:512], lhsT=VV[s][:, i, 0:65], rhs=Pb[pslot][:, j, q0:512], start=first, stop=last),
                            r=[f"VV{s}", f"P{pslot}"], w=[f"ps{ob}"])

                def pair(s, rows, J, blks, scale, ob):
                    n = state["n"]
                    state["n"] += 1
                    sp_ = SPAIR[n % 2]
                    pslot = n % 3
                    r0, r1_ = rows
                    for j, (i, q0, mask, first, last) in enumerate(blks):
                        tr.op("pe", lambda e, i=i, q0=q0, j=j: e.matmul(
                            ps[:, sp_[j], q0:512], lhsT=KT[s][r0:r1_, i * 128:(i + 1) * 128],
                            rhs=QT[s][r0:r1_, J * 512 + q0:(J + 1) * 512], start=True, stop=True),
                            r=[f"KT{s}", f"QT{s}"], w=[f"ps{sp_[j]}"])
                    qm = blks[0][1]
                    nb = len(blks)
                    tr.op("act", lambda e: e.activation(out=Pb[pslot][:, 0:nb, qm:512], in_=ps[:, sp_[0]:sp_[0] + nb, qm:512],
                                                        func=AF.Exp, scale=scale),
                          r=[f"ps{sp_[j]}" for j in range(nb)], w=[f"P{pslot}"])
                    grp = []
                    for j, (i, q0, mask, first, last) in enumerate(blks):
                        if mask is not None:
                            mk, mkey, w_ = mask
                            tr.op("dve", lambda e, j=j, q0=q0, w_=w_, mk=mk: e.tensor_tensor(
                                out=Pb[pslot][:, j, q0:q0 + w_], in0=Pb[pslot][:, j, q0:q0 + w_], in1=mk, op=ALU.mult),
                                r=[f"P{pslot}", mkey], w=[f"P{pslot}"])
                        grp.append((s, i, q0, ob, first, last, pslot, j))
                    blocks.append(grp)
                    if len(blocks) > 1:
                        emit_pv(blocks.pop(0))
                    tick()

                def run_blocks(s, rows, J, lst, scale, ob):
                    for k in range(0, len(lst), 2):
                        pair(s, rows, J, lst[k:k + 2], scale, ob)

                def drain_pv():
                    while blocks:
                        emit_pv(blocks.pop(0))

                def post_simple(ob, row0, J):
                    k = state["npost"]
                    state["npost"] += 1
                    o_ = k % 2

                    def s1():
                        tr.op("dve", lambda e: e.reciprocal(out=rden[64:65, 0:512], in_=ps[64:65, ob, :]),
                              r=[f"ps{ob}"], w=["rden"])

                    def s2():
                        tr.op("pe", lambda e: e.matmul(ps[0:64, MISC, :], lhsT=ones32[64:65, 0:64], rhs=rden[64:65, 0:512],
                                                       start=True, stop=True), r=["rden", "ones32"], w=[f"ps{MISC}"])
                        tr.op("dve", lambda e: e.tensor_copy(out=bc[0][:], in_=ps[0:64, MISC, :]), r=[f"ps{MISC}"], w=["bc0"])
                        tr.op("dve", lambda e: e.tensor_tensor(out=ost[o_][:], in0=ps[0:64, ob, :], in1=bc[0][:], op=ALU.mult),
                              r=[f"ps{ob}", "bc0"], w=[f"ost{o_}"])
                        tr.dma("sp", oT_d[row0:row0 + 64, J * 512:(J + 1) * 512], ost[o_][:], f"ost{o_}", r=[f"ost{o_}"])

                    s1()
                    defer(2, s2)

                def post_A(ob0, ob1, row0, J):
                    k = state["npost"]
                    state["npost"] += 1
                    o_ = k % 2

                    def s1():
                        tr.op("dve", lambda e: e.reciprocal(out=rden[64:65, 0:512], in_=ps[64:65, ob0, :]),
                              r=[f"ps{ob0}"], w=["rden"])
                        tr.op("dve", lambda e: e.reciprocal(out=rden[64:65, 512:1024], in_=ps[64:65, ob1, :]),
                              r=[f"ps{ob1}"], w=["rden"])

                    def s2():
                        for m, ob in ((0, ob0), (1, ob1)):
                            tr.op("pe", lambda e, m=m: e.matmul(ps[0:64, MISC, :], lhsT=ones32[64:65, 0:64],
                                                                rhs=rden[64:65, m * 512:(m + 1) * 512], start=True, stop=True),
                                  r=["rden", "ones32"], w=[f"ps{MISC}"])
                            tr.op("dve", lambda e, m=m: e.tensor_copy(out=bc[m][:], in_=ps[0:64, MISC, :]),
                                  r=[f"ps{MISC}"], w=[f"bc{m}"])
                            tr.op("dve", lambda e, m=m, ob=ob: e.tensor_tensor(out=tA[m][:], in0=ps[0:64, ob, :], in1=bc[m][:],
                                                                               op=ALU.mult),
                                  r=[f"ps{ob}", f"bc{m}"], w=[f"tA{m}"])
                        tr.op("dve", lambda e: e.scalar_tensor_tensor(out=tA[2][:], in0=tA[1][:], scalar=sm[0:64, 0:1], in1=tA[0][:],
                                                                      op0=ALU.mult, op1=ALU.add),
                              r=["tA0", "tA1", "sm"], w=["tA2"])
                        tr.op("pool", lambda e: e.tensor_tensor(out=tA[0][:], in0=tA[2][:], in1=tA[2][:], op=ALU.mult),
                              r=["tA2"], w=["tA0"])

                    def s3():
                        tr.op("pe", lambda e: e.matmul(ps[0:64, MISC, :], lhsT=ones32[0:64, 0:64], rhs=tA[0][:],
                                                       start=True, stop=True), r=["tA0", "ones32"], w=[f"ps{MISC}"])
                        tr.op("dve", lambda e: e.tensor_scalar(out=tA[1][:], in0=ps[0:64, MISC, :], scalar1=1.0 / 64, scalar2=SUBLN_EPS,
                                                               op0=ALU.mult, op1=ALU.add), r=[f"ps{MISC}"], w=["tA1"])

                    def s4():
                        tr.op("act", lambda e: e.activation(out=tA[1][:], in_=tA[1][:], func=AF.Ln), r=["tA1"], w=["tA1"])
                        tr.op("act", lambda e: e.activation(out=tA[1][:], in_=tA[1][:], func=AF.Exp, scale=-0.5), r=["tA1"], w=["tA1"])
                        tr.op("dve", lambda e: e.scalar_tensor_tensor(out=ost[o_][:], in0=tA[2][:], scalar=sm[0:64, 1:2], in1=tA[1][:],
                                                                      op0=ALU.mult, op1=ALU.mult),
                              r=["tA2", "tA1", "sm"], w=[f"ost{o_}"])
                        tr.dma("sp", oT_d[row0:row0 + 64, J * 512:(J + 1) * 512], ost[o_][:], f"ost{o_}", r=[f"ost{o_}"])

                    s1()
                    defer(2, s2)
                    defer(4, s3)
                    defer(6, s4)

                load_unit(0)
                for u, (kind, h) in enumerate(units):
                    s = u % 2
                    if u + 1 < len(units):
                        load_unit(u + 1)
                    for J in range(NQ):
                        nk = 4 * J + 4

                        def mk_list(i0, maskfn):
                            lst = []
                            for i in range(i0, nk):
                                a = i - 4 * J
                                q0 = 128 * a if a > 0 else 0
                                lst.append((i, q0, maskfn(i, a, q0), i == i0, i == nk - 1))
                            return lst

                        def nxt_ob():
                            ob = OB[state["no"] % 3]
                            state["no"] += 1
                            return ob

                        if kind == "A":
                            obs = []
                            for m in range(2):
                                ob = nxt_ob()
                                obs.append(ob)
                                lst = mk_list(0, lambda i, a, q0: (tri[:, :], "tri", 128) if a >= 0 else None)
                                run_blocks(s, (32 * m, 32 * m + 32), J, lst, 32 ** -0.5, ob)
                            drain_pv()
                            post_A(obs[0], obs[1], h * 64, J)
                        elif kind == "C":
                            ob = nxt_ob()
                            lst = mk_list(0, lambda i, a, q0: (tri[:, :], "tri", 128) if a >= 0 else None)
                            run_blocks(s, (0, 68), J, lst, 1.0, ob)
                            drain_pv()
                            post_simple(ob, 256 + 384 + h * 64, J)
                        else:
                            ob = nxt_ob()
                            lst = mk_list(max(0, 4 * J - 16),
                                          lambda i, a, q0: (mB[:, (4 * J - i) + 3, q0:512], "mB", 512 - q0))
                            run_blocks(s, (0, 64), J, lst, 0.125, ob)
                            drain_pv()
                            post_simple(ob, 256 + h * 64, J)
                flush()
                tr.barrier()

        def phase3(l, x_src):
            with contextlib.ExitStack() as es:
                wout = sbuf(es, "wout", [128, 8, D], BF16)
                wup = sbuf(es, "wup", [128, 8, 2 * DFF], BF16)
                cw = sbuf(es, "cw", [128, NFF, 4], F32)
                oTt = sbuf(es, "oTt", [128, 8, 512], BF16)
                xin = [sbuf(es, f"xin{i}", [128, D], F32) for i in range(2)]
                junk = sbuf(es, "junk", [128, D], BF16)
                ss = sbuf(es, "ss", [128, 4], F32)
                t32 = sbuf(es, "t32", [128, D], F32)
                hb = sbuf(es, "hb", [128, D], BF16)
                hT = sbuf(es, "hT", [128, 8, 512], BF16)
                gbuf = [sbuf(es, f"gbuf{i}", [128, 514], F32) for i in range(2)]
                acc = [sbuf(es, f"acc{i}", [128, 512], F32) for i in range(2)]
                sg = [sbuf(es, f"sg{i}", [128, 512], F32) for i in range(2)]
                hal = sbuf(es, "hal", [128, NFF, 2], F32)
                ast = [sbuf(es, f"ast{i}", [128, 512], BF16) for i in range(4)]

                load_w_bf16(wout, w_out[l].rearrange("(kc p) n -> p kc n", p=128), 8, "wout", "wl")
                load_w_bf16(wup, w_up[l].rearrange("(kc p) n -> p kc n", p=128), 8, "wup", "wl")
                for k in range(3):
                    tr.dma("sp", cw[:, :, k:k + 1], conv_w[l, k:k + 1, :].rearrange("o (f p) -> p f o", p=128), "c0", w=["cw"],
                           allow_slow_non_contiguous=True)
                tr.dma("sp", cw[:, :, 3:4], conv_b[l:l + 1, :].rearrange("o (f p) -> p f o", p=128), "c0", w=["cw"],
                       allow_slow_non_contiguous=True)
                tr.op("dve", lambda e: e.memset(hal[:], 0.0), w=["hal"])

                nx = 0
                for tt in range(NQ):
                    t0 = tt * 512
                    tr.dma("sp", oTt[:], oT_d[:, t0:t0 + 512].rearrange("(kc p) t -> p kc t", p=128), "oTt", w=["oTt"])
                    for s in range(4):
                        xs = nx % 2
                        nx += 1
                        xk = f"xin{xs}"
                        tr.dma("sp", xin[xs][:], x_src[t0 + s * 128:t0 + (s + 1) * 128, :], xk, w=[xk])
                        for j in range(2):
                            b = 2 + j
                            for kc in range(8):
                                tr.op("pe", lambda e, kc=kc, j=j, b=b, s=s: e.matmul(
                                    PS(b), lhsT=oTt[:, kc, s * 128:(s + 1) * 128], rhs=wout[:, kc, j * 512:(j + 1) * 512],
                                    start=(kc == 0), stop=(kc == 7)), r=["oTt", "wout"], w=[f"ps{b}"])
                            tr.op("dve", lambda e, j=j, b=b: e.tensor_tensor(
                                out=t32[:, j * 512:(j + 1) * 512], in0=PS(b), in1=mod[:, 2 * D + j * 512:2 * D + (j + 1) * 512],
                                op=ALU.mult), r=[f"ps{b}", "mod"], w=["t32"])
                        tr.op("pool", lambda e, xs=xs: e.tensor_tensor(out=xin[xs][:], in0=xin[xs][:], in1=t32[:], op=ALU.add),
                              r=["t32", xk], w=[xk])
                        tr.dma("sp", xres[t0 + s * 128:t0 + (s + 1) * 128, :], xin[xs][:], xk, r=[xk])
                        rms_to_hT((junk, ss, t32, hb), xin[xs][:], xk, s, 4 * D, 3 * D, hT, "hT", s % 2, "p3")
                    for f in range(NFF):
                        bu = 4 + (f % 2)
                        bg = 6 + (f % 2)
                        z = f % 2
                        for kc in range(8):
                            tr.op("pe", lambda e, kc=kc, f=f, bu=bu: e.matmul(
                                PS(bu), lhsT=wup[:, kc, f * 128:(f + 1) * 128], rhs=hT[:, kc, :], start=(kc == 0), stop=(kc == 7)),
                                r=["wup", "hT"], w=[f"ps{bu}"])
                        for kc in range(8):
                            tr.op("pe", lambda e, kc=kc, f=f, bg=bg: e.matmul(
                                PS(bg), lhsT=wup[:, kc, DFF + f * 128:DFF + (f + 1) * 128], rhs=hT[:, kc, :],
                                start=(kc == 0), stop=(kc == 7)), r=["wup", "hT"], w=[f"ps{bg}"])
                        gk = f"gbuf{z}"
                        tr.op("pool", lambda e, z=z, f=f: e.tensor_copy(out=gbuf[z][:, 0:2], in_=hal[:, f, :]), r=["hal"], w=[gk])
                        tr.op("act", lambda e, z=z, bg=bg: e.activation(out=gbuf[z][:, 2:514], in_=PS(bg), func=AF.Copy),
                              r=[f"ps{bg}"], w=[gk])
                        tr.op("pool", lambda e, z=z, f=f: e.tensor_copy(out=hal[:, f, :], in_=gbuf[z][:, 512:514]), r=[gk], w=["hal"])
                        ak = f"acc{z}"
                        tr.op("dve", lambda e, z=z, f=f: e.tensor_scalar(
                            out=acc[z][:], in0=gbuf[z][:, 2:514], scalar1=cw[:, f, 2:3], scalar2=cw[:, f, 3:4],
                            op0=ALU.mult, op1=ALU.add), r=[gk, "cw"], w=[ak])
                        tr.op("dve", lambda e, z=z, f=f: e.scalar_tensor_tensor(
                            out=acc[z][:], in0=gbuf[z][:, 1:513], scalar=cw[:, f, 1:2], in1=acc[z][:], op0=ALU.mult, op1=ALU.add),
                            r=[gk, "cw", ak], w=[ak])
                        tr.op("dve", lambda e, z=z, f=f: e.scalar_tensor_tensor(
                            out=acc[z][:], in0=gbuf[z][:, 0:512], scalar=cw[:, f, 0:1], in1=acc[z][:], op0=ALU.mult, op1=ALU.add),
                            r=[gk, "cw", ak], w=[ak])
                        sk = f"sg{z}"
                        tr.op("act", lambda e, z=z: e.activation(out=sg[z][:], in_=acc[z][:], func=AF.Silu), r=[ak], w=[sk])
                        a4 = f % 4
                        tr.op("dve", lambda e, z=z, a4=a4, bu=bu: e.tensor_tensor(out=ast[a4][:], in0=PS(bu), in1=sg[z][:], op=ALU.mult),
                              r=[f"ps{bu}", sk], w=[f"ast{a4}"])
                        tr.dma("sp", aT_d[f * 128:(f + 1) * 128, t0:t0 + 512], ast[a4][:], f"ast{a4}", r=[f"ast{a4}"])
                tr.barrier()

        def phase4(l, dst, do_final, xdst=None):
            with contextlib.ExitStack() as es:
                wdn = sbuf(es, "wdn", [128, NFF, D], BF16)
                aTt = [sbuf(es, f"aTt{i}", [128, NFF, 512], BF16) for i in range(2)]
                xin = [sbuf(es, f"xin{i}", [128, D], F32) for i in range(3)]
                t32 = sbuf(es, "t32", [128, D], F32)
                junk = sbuf(es, "junk", [128, D], BF16)
                ss = sbuf(es, "ss", [128, 4], F32)
                gf = sbuf(es, "gf", [128, D], F32)
                load_w_bf16(wdn, w_down[l].rearrange("(f p) n -> p f n", p=128), NFF, "wdn", "wl")
                if do_final:
                    tr.dma("sp", gf[:], g_fin[0:1, :].partition_broadcast(128), "c0", w=["gf"])
                nx = 0
                for tt in range(NQ):
                    t0 = tt * 512
                    a_ = tt % 2
                    akey = f"aTt{a_}"
                    tr.dma("sp", aTt[a_][:], aT_d[:, t0:t0 + 512].rearrange("(f p) t -> p f t", p=128), akey, w=[akey])
                    for s in range(4):
                        xs = nx % 3
                        nx += 1
                        xk = f"xin{xs}"
                        tr.dma("sp", xin[xs][:], xres[t0 + s * 128:t0 + (s + 1) * 128, :], xk, w=[xk])
                        for j in range(2):
                            b = (2 * s + j) % 4
                            for f in range(NFF):
                                tr.op("pe", lambda e, f=f, j=j, b=b, s=s, a_=a_: e.matmul(
                                    PS(b), lhsT=aTt[a_][:, f, s * 128:(s + 1) * 128], rhs=wdn[:, f, j * 512:(j + 1) * 512],
                                    start=(f == 0), stop=(f == NFF - 1)), r=[akey, "wdn"], w=[f"ps{b}"])
                            tr.op("dve", lambda e, j=j, b=b: e.tensor_tensor(
                                out=t32[:, j * 512:(j + 1) * 512], in0=PS(b), in1=mod[:, 5 * D + j * 512:5 * D + (j + 1) * 512],
                                op=ALU.mult), r=[f"ps{b}", "mod"], w=["t32"])
                        tr.op("pool", lambda e, xs=xs: e.tensor_tensor(out=xin[xs][:], in0=xin[xs][:], in1=t32[:], op=ALU.add),
                              r=["t32", xk], w=[xk])
                        if do_final:
                            tr.op("act", lambda e, xs=xs: e.activation(out=junk[:], in_=xin[xs][:], func=AF.Square, accum_out=ss[:, 0:1]),
                                  r=[xk], w=["junk", "ss"])
                            tr.op("dve", lambda e: e.tensor_scalar(out=ss[:, 1:2], in0=ss[:, 0:1], scalar1=1.0 / D, scalar2=NORM_EPS,
                                                                   op0=ALU.mult, op1=ALU.add), r=["ss"], w=["ss"])
                            tr.op("act", lambda e: e.activation(out=ss[:, 2:3], in_=ss[:, 1:2], func=AF.Sqrt), r=["ss"], w=["ss"])
                            tr.op("dve", lambda e: e.reciprocal(out=ss[:, 3:4], in_=ss[:, 2:3]), r=["ss"], w=["ss"])
                            if xdst is not None:
                                tr.dma("sp", xdst[t0 + s * 128:t0 + (s + 1) * 128, :], xin[xs][:], xk, r=[xk])
                            tr.op("dve", lambda e, xs=xs: e.scalar_tensor_tensor(
                                out=t32[:], in0=xin[xs][:], scalar=ss[:, 3:4], in1=gf[:], op0=ALU.mult, op1=ALU.mult),
                                r=[xk, "ss", "gf"], w=["t32"])
                            tr.dma("sp", dst[t0 + s * 128:t0 + (s + 1) * 128, :], t32[:], "t32st", r=["t32"])
                        else:
                            tr.dma("sp", dst[t0 + s * 128:t0 + (s + 1) * 128, :], xin[xs][:], xk, r=[xk])
                tr.barrier()

        for l in range(L):
            phase0(l)
            phase1(l, x_d if l == 0 else xres)
            phase2(l)
            phase3(l, x_d if l == 0 else xres)
            last = (l == L - 1)
            phase4(l, y_d if last else xres, last and final, xo_d if last else None)
        tr.barrier()
        print("kernel instructions (approx):", tr.ninst, flush=True)
    return nc


def _rope_tab(T, dim, theta=10000.0):
    inv = (1.0 / (np.float32(theta) ** (np.arange(0, dim, 2, dtype=np.float32) / np.float32(dim)))).astype(np.float32)
    ang = (np.arange(T, dtype=np.float32)[:, None] * inv[None, :]).astype(np.float32)
    cos = np.cos(ang).astype(np.float32).T
    sin = np.sin(ang).astype(np.float32).T
    tab = np.empty((2, dim, T), np.float32)
    tab[0] = np.concatenate([cos, cos], 0)
    tab[1] = np.concatenate([-sin, sin], 0)
    return tab


def _consts(T):
    kk = np.arange(128)[:, None]
    qq = np.arange(512)[None, :]
    mB = np.zeros((20, 128, 512), np.float32)
    for di in range(20):
        dist = (di - 3) * 128 + qq - kk
        m = ((dist >= 0) & (dist <= 128)).astype(np.float32)
        m += ((dist >= 0) & (dist <= 512) & (dist % 4 == 0)).astype(np.float32)
        m += ((dist >= 0) & (dist <= 2048) & (dist % 16 == 0)).astype(np.float32)
        mB[di] = m
    tri = (np.arange(128)[:, None] <= np.arange(128)[None, :]).astype(np.float32)
    return dict(
        ropeA=_rope_tab(T, 32), ropeB=_rope_tab(T, 64),
        maskB=mB.astype(ml_dtypes.bfloat16), tri=tri.astype(ml_dtypes.bfloat16),
        identb=np.eye(128, dtype=np.float32).astype(ml_dtypes.bfloat16),
    )


def _rot_perm():
    idx = []
    for base in (QA0, KA0):
        for u in range(8):
            for d in range(32):
                idx.append(base + u * 32 + (d + 16) % 32)
    for base in (QB0, KB0):
        for h in range(6):
            for d in range(64):
                idx.append(base + h * 64 + (d + 32) % 64)
    return np.asarray(idx, np.int64)


def make_in_maps(inputs, T, layers, ncores):
    f = lambda a: np.ascontiguousarray(np.asarray(a, dtype=np.float32))
    L = len(layers)
    ls = list(layers)
    perm = _rot_perm()
    w_in = f(inputs["w_in"])[ls]
    shared = dict(
        w_mod=f(inputs["w_mod"])[ls], b_mod=f(inputs["b_mod"])[ls], g_attn=f(inputs["g_attn"])[ls],
        w_in=np.ascontiguousarray(w_in), w_inr=np.ascontiguousarray(w_in[:, :, perm]),
        diff_lambda=f(inputs["diff_lambda"])[ls].reshape(L, 128), subln_g=f(inputs["subln_g"])[ls],
        forget_bias=f(inputs["forget_bias"])[ls], w_out=f(inputs["w_out"])[ls], g_mlp=f(inputs["g_mlp"])[ls],
        w_up=f(inputs["w_up"])[ls], conv_w=f(inputs["conv_w"])[ls], conv_b=f(inputs["conv_b"])[ls],
        w_down=f(inputs["w_down"])[ls], g_final=f(inputs["g_final"]).reshape(1, D),
        lamc=np.asarray([[0.8 - 0.6 * math.exp(-0.3 * l), 1.0 - (0.8 - 0.6 * math.exp(-0.3 * l))] for l in ls], np.float32),
    )
    shared.update(_consts(T))
    x = np.asarray(inputs["x"], dtype=np.float32)
    c = np.asarray(inputs["c"], dtype=np.float32)
    maps = []
    for b in range(ncores):
        m = dict(shared)
        m["x"] = np.ascontiguousarray(x[b, :T])
        m["c"] = np.ascontiguousarray(c[b])
        maps.append(m)
    return maps


FUSED = False


def kernel(x, c, w_mod, b_mod, g_attn, w_in, diff_lambda, subln_g, forget_bias, w_out,
           g_mlp, w_up, conv_w, conv_b, w_down, g_final):
    inputs = dict(x=x, c=c, w_mod=w_mod, b_mod=b_mod, g_attn=g_attn, w_in=w_in, diff_lambda=diff_lambda,
                  subln_g=subln_g, forget_bias=forget_bias, w_out=w_out, g_mlp=g_mlp, w_up=w_up,
                  conv_w=conv_w, conv_b=conv_b, w_down=w_down, g_final=g_final)
    B, T, _ = np.asarray(x).shape
    L = np.asarray(w_in).shape[0]
    assert B == NCORES
    if FUSED:
        nc = build(T, L, final=True)
        maps = make_in_maps(inputs, T, range(L), NCORES)
        res = run_bass_kernel_spmd(nc, maps, core_ids=list(range(NCORES)))
        return np.stack([np.asarray(r["y"], dtype=np.float32) for r in res.results], axis=0)
    nc_mid = build(T, 1, final=False) if L > 1 else None
    nc_last = build(T, 1, final=True)
    cur = np.asarray(x, dtype=np.float32)
    for l in range(L):
        inputs["x"] = cur
        maps = make_in_maps(inputs, T, [l], NCORES)
        res = run_bass_kernel_spmd(nc_last if l == L - 1 else nc_mid, maps, core_ids=list(range(NCORES)))
        cur = np.stack([np.asarray(r["y"], dtype=np.float32) for r in res.results], axis=0)
    return cur
```
